# Optimizing a Trainium2 kernel written in Bass

```python
import math
import jax, jax.numpy as jnp
from jax import lax
import numpy as np

D_MODEL = 1024
BATCH = 1
SEQ = 16384
DEPTH = 1

ATTN_WIDTH = D_MODEL // 2
SSM_WIDTH = D_MODEL - ATTN_WIDTH
HEAD_DIM = 64
N_DIFF_HEADS = ATTN_WIDTH // (2 * HEAD_DIM)
V_DIM = 2 * HEAD_DIM
QK_WIDTH = N_DIFF_HEADS * 2 * HEAD_DIM
V_WIDTH = N_DIFF_HEADS * V_DIM
ROT_DIM = HEAD_DIM // 4
ROPE_THETA = 500000.0
Q_BLOCK = 128
SSM_GROUP = 16
N_SSM_GROUPS = SSM_WIDTH // SSM_GROUP
SSM_STATE = 64
DT_MIN = 1e-3
DT_MAX = 1e-1
IN_WIDTH = 2 * QK_WIDTH + V_WIDTH + SSM_WIDTH
D_FF = ((-(-8 * D_MODEL // 3) + 255) // 256) * 256
NORM_EPS = 1e-5

kernel_name = "hymba_s5_diffattn_layer"


def rms_norm(x, g):
    x32 = x.astype(jnp.float32)
    y = x32 * lax.rsqrt(jnp.mean(x32 * x32, axis=-1, keepdims=True) + NORM_EPS)
    return (y * g.astype(jnp.float32)).astype(x.dtype)


def rope_tables(positions):
    inv_freq = ROPE_THETA ** (-jnp.arange(0, ROT_DIM, 2, dtype=jnp.float32) / ROT_DIM)
    ang = positions.astype(jnp.float32)[..., None] * inv_freq
    ang = jnp.concatenate([ang, ang], axis=-1)
    return jnp.cos(ang), jnp.sin(ang)


def partial_rope(t, cos, sin):
    half = ROT_DIM // 2
    rot = t[..., :ROT_DIM].astype(jnp.float32)
    c = cos[:, :, None, None, :]
    s = sin[:, :, None, None, :]
    rotated = rot * c + jnp.concatenate([-rot[..., half:], rot[..., :half]], axis=-1) * s
    return jnp.concatenate([rotated.astype(t.dtype), t[..., ROT_DIM:]], axis=-1)


def diff_attention(q, k, v, lam):
    b, s_len, h, e = v.shape
    n_blocks = s_len // Q_BLOCK
    scale = HEAD_DIM ** -0.5
    k_pos = jnp.arange(s_len)

    def block(i):
        start = i * Q_BLOCK
        qb = lax.dynamic_slice_in_dim(q, start, Q_BLOCK, axis=1)
        sc = jnp.einsum('bqhcd,bkhcd->bhcqk', qb, k).astype(jnp.float32) * scale
        q_pos = start + jnp.arange(Q_BLOCK)
        causal = k_pos[None, :] <= q_pos[:, None]
        sc = jnp.where(causal, sc, -jnp.inf)
        p = jax.nn.softmax(sc, axis=-1)
        a = p[:, :, 0] - lam * p[:, :, 1]
        return jnp.einsum('bhqk,bkhe->bqhe', a.astype(v.dtype), v)

    out = lax.map(block, jnp.arange(n_blocks))
    return jnp.moveaxis(out, 0, 1).reshape(b, s_len, h, e)


def s5_mixer(u, lam_re, lam_im, log_step, b_re, b_im, c_re, c_im, d, w_glu, b_glu):
    bsz, s_len, _ = u.shape
    f32 = jnp.float32
    u32 = u.astype(f32).reshape(bsz, s_len, N_SSM_GROUPS, SSM_GROUP)
    lr = jnp.minimum(lam_re.astype(f32), -1e-4)
    li = lam_im.astype(f32)
    step = jnp.exp(log_step.astype(f32))[:, None]
    mag = jnp.exp(lr * step)
    lb_re = mag * jnp.cos(li * step)
    lb_im = mag * jnp.sin(li * step)
    denom = lr * lr + li * li
    nr = lb_re - 1.0
    ni = lb_im
    coef_re = (nr * lr + ni * li) / denom
    coef_im = (ni * lr - nr * li) / denom
    br = b_re.astype(f32)
    bi = b_im.astype(f32)
    bb_re = coef_re[..., None] * br - coef_im[..., None] * bi
    bb_im = coef_re[..., None] * bi + coef_im[..., None] * br
    bu_re = jnp.einsum('gph,bsgh->bsgp', bb_re, u32)
    bu_im = jnp.einsum('gph,bsgh->bsgp', bb_im, u32)
    a_re = jnp.broadcast_to(lb_re, bu_re.shape)
    a_im = jnp.broadcast_to(lb_im, bu_im.shape)

    def combine(e1, e2):
        a1r, a1i, b1r, b1i = e1
        a2r, a2i, b2r, b2i = e2
        return (a2r * a1r - a2i * a1i,
                a2r * a1i + a2i * a1r,
                a2r * b1r - a2i * b1i + b2r,
                a2r * b1i + a2i * b1r + b2i)

    _, _, x_re, x_im = lax.associative_scan(combine, (a_re, a_im, bu_re, bu_im), axis=1)
    y = (jnp.einsum('ghp,bsgp->bsgh', c_re.astype(f32), x_re)
         - jnp.einsum('ghp,bsgp->bsgh', c_im.astype(f32), x_im)
         + d.astype(f32) * u32)
    y = jax.nn.gelu(y.reshape(bsz, s_len, SSM_WIDTH))
    y = y * jax.nn.sigmoid(y @ w_glu.astype(f32) + b_glu.astype(f32))
    return y.astype(u.dtype)


def setup_inputs(seed: int = 0) -> dict:
    key = jax.random.key(seed)
    ks = jax.random.split(key, 24)
    f32 = jnp.float32
    L, G, P, H = DEPTH, N_SSM_GROUPS, SSM_STATE, SSM_GROUP

    def nrm(k, shape, scale):
        return jax.random.normal(k, shape, f32) * scale

    x = jax.random.normal(ks[0], (BATCH, SEQ, D_MODEL), f32)
    positions = jnp.broadcast_to(jnp.arange(SEQ, dtype=jnp.int32), (BATCH, SEQ))
    norm1_g = 1.0 + nrm(ks[1], (L, D_MODEL), 0.02)
    w_in = nrm(ks[2], (L, D_MODEL, IN_WIDTH), D_MODEL ** -0.5)
    lambda_q1 = nrm(ks[3], (L, HEAD_DIM), 0.1)
    lambda_k1 = nrm(ks[4], (L, HEAD_DIM), 0.1)
    lambda_q2 = nrm(ks[5], (L, HEAD_DIM), 0.1)
    lambda_k2 = nrm(ks[6], (L, HEAD_DIM), 0.1)
    subln_g = 1.0 + nrm(ks[7], (L, V_DIM), 0.02)
    ssm_lambda_re = -0.5 + nrm(ks[8], (L, G, P), 0.01)
    ssm_lambda_im = math.pi * jnp.arange(P, dtype=f32) + nrm(ks[9], (L, G, P), 0.01)
    ssm_log_step = jax.random.uniform(ks[10], (L, G), f32, math.log(DT_MIN), math.log(DT_MAX))
    ssm_b_re = nrm(ks[11], (L, G, P, H), (2 * H) ** -0.5)
    ssm_b_im = nrm(ks[12], (L, G, P, H), (2 * H) ** -0.5)
    ssm_c_re = nrm(ks[13], (L, G, H, P), P ** -0.5)
    ssm_c_im = nrm(ks[14], (L, G, H, P), P ** -0.5)
    ssm_d = nrm(ks[15], (L, G, H), 1.0)
    ssm_w_glu = nrm(ks[16], (L, SSM_WIDTH, SSM_WIDTH), SSM_WIDTH ** -0.5)
    ssm_b_glu = nrm(ks[17], (L, SSM_WIDTH), 0.01)
    w_out = nrm(ks[18], (L, D_MODEL, D_MODEL), D_MODEL ** -0.5)
    norm2_g = 1.0 + nrm(ks[19], (L, D_MODEL), 0.02)
    w_gate = nrm(ks[20], (L, D_MODEL, D_FF), D_MODEL ** -0.5)
    w_up = nrm(ks[21], (L, D_MODEL, D_FF), D_MODEL ** -0.5)
    w_down = nrm(ks[22], (L, D_FF, D_MODEL), D_FF ** -0.5)
    final_g = 1.0 + nrm(ks[23], (D_MODEL,), 0.02)
    return {"x": x, "positions": positions, "norm1_g": norm1_g, "w_in": w_in,
            "lambda_q1": lambda_q1, "lambda_k1": lambda_k1, "lambda_q2": lambda_q2,
            "lambda_k2": lambda_k2, "subln_g": subln_g,
            "ssm_lambda_re": ssm_lambda_re, "ssm_lambda_im": ssm_lambda_im,
            "ssm_log_step": ssm_log_step, "ssm_b_re": ssm_b_re, "ssm_b_im": ssm_b_im,
            "ssm_c_re": ssm_c_re, "ssm_c_im": ssm_c_im, "ssm_d": ssm_d,
            "ssm_w_glu": ssm_w_glu, "ssm_b_glu": ssm_b_glu, "w_out": w_out,
            "norm2_g": norm2_g, "w_gate": w_gate, "w_up": w_up, "w_down": w_down,
            "final_g": final_g}


def reference(x, positions, norm1_g, w_in, lambda_q1, lambda_k1, lambda_q2, lambda_k2,
              subln_g, ssm_lambda_re, ssm_lambda_im, ssm_log_step, ssm_b_re, ssm_b_im,
              ssm_c_re, ssm_c_im, ssm_d, ssm_w_glu, ssm_b_glu, w_out, norm2_g,
              w_gate, w_up, w_down, final_g):
    bsz, s_len, _ = x.shape
    cos, sin = rope_tables(positions)
    h = x
    for l in range(DEPTH):
        lambda_init = 0.8 - 0.6 * math.exp(-0.3 * l)
        hn = rms_norm(h, norm1_g[l])
        proj = hn @ w_in[l]
        q, k, v, u = jnp.split(proj, [QK_WIDTH, 2 * QK_WIDTH, 2 * QK_WIDTH + V_WIDTH], axis=-1)
        q = partial_rope(q.reshape(bsz, s_len, N_DIFF_HEADS, 2, HEAD_DIM), cos, sin)
        k = partial_rope(k.reshape(bsz, s_len, N_DIFF_HEADS, 2, HEAD_DIM), cos, sin)
        v = v.reshape(bsz, s_len, N_DIFF_HEADS, V_DIM)
        lam = (jnp.exp(jnp.sum(lambda_q1[l].astype(jnp.float32) * lambda_k1[l].astype(jnp.float32)))
               - jnp.exp(jnp.sum(lambda_q2[l].astype(jnp.float32) * lambda_k2[l].astype(jnp.float32)))
               + lambda_init)
        attn = diff_attention(q, k, v, lam)
        attn = (rms_norm(attn, subln_g[l]) * (1.0 - lambda_init)).reshape(bsz, s_len, ATTN_WIDTH)
        ssm = s5_mixer(u, ssm_lambda_re[l], ssm_lambda_im[l], ssm_log_step[l],
                       ssm_b_re[l], ssm_b_im[l], ssm_c_re[l], ssm_c_im[l], ssm_d[l],
                       ssm_w_glu[l], ssm_b_glu[l])
        mixed = jnp.concatenate([attn, ssm.astype(attn.dtype)], axis=-1) @ w_out[l]
        h = h + mixed
        hn = rms_norm(h, norm2_g[l])
        h = h + (jax.nn.silu(hn @ w_gate[l]) * (hn @ w_up[l])) @ w_down[l]
    return rms_norm(h, final_g)
```

```python
import contextlib
import math

import numpy as np
import ml_dtypes

import concourse.bass as bass
import concourse.mybir as mybir
from concourse.bass_utils import run_bass_kernel_spmd

F32, BF16, I32 = mybir.dt.float32, mybir.dt.bfloat16, mybir.dt.int32
AF = mybir.ActivationFunctionType
ALU = mybir.AluOpType
AX = mybir.AxisListType

NCORES = 8
D = 1024
DFF = 2816
NFF = DFF // 128
EPS = 1e-5
NEG = -30000.0
LAMBDA_INIT = 0.8 - 0.6 * math.exp(-0.3 * 0)
TWO_PI = 2.0 * math.pi
C1 = 6.28125
C2 = TWO_PI - C1
PI_SAFE = 3.141592
GELU_K1 = 2.0 * math.sqrt(2.0 / math.pi)
GELU_K2 = GELU_K1 * 0.044715

ENGS = ("pe", "act", "dve", "pool", "sp")


class _Op:
    __slots__ = ("eng", "fn", "reads", "writes", "dma_sem", "idx", "eidx", "deps",
                 "flag", "semval", "dma_cnt", "dmawait", "is_bar")

    def __init__(self, eng, fn, reads, writes, dma_sem):
        self.eng, self.fn, self.reads, self.writes, self.dma_sem = eng, fn, reads, writes, dma_sem
        self.deps = []
        self.flag = False
        self.semval = 0
        self.dma_cnt = 0
        self.dmawait = {}
        self.is_bar = False


class Prog:
    def __init__(self, nc):
        self.nc = nc
        self.ops = []
        self.per_eng = {e: [] for e in ENGS}
        self.last_w = {}
        self.readers = {}
        self.dma_counts = {}

    def op(self, eng, fn, reads=(), writes=(), dma_sem=None):
        o = _Op(eng, fn, tuple(reads), tuple(writes), dma_sem)
        o.idx = len(self.ops)
        o.eidx = len(self.per_eng[eng])
        deps = set()
        for k in o.reads:
            w = self.last_w.get(k)
            if w is not None:
                deps.add(w)
        for k in o.writes:
            w = self.last_w.get(k)
            if w is not None:
                deps.add(w)
            for r in self.readers.get(k, ()):
                deps.add(r)
        deps.discard(o)
        o.deps = sorted(deps, key=lambda d: d.idx)
        o.dmawait = {d.dma_sem: 16 * self.dma_counts[d.dma_sem] for d in o.deps if d.dma_sem is not None}
        for k in o.writes:
            self.last_w[k] = o
            self.readers[k] = []
        for k in o.reads:
            self.readers.setdefault(k, []).append(o)
        if dma_sem is not None:
            c = self.dma_counts.get(dma_sem, 0) + 1
            self.dma_counts[dma_sem] = c
            o.dma_cnt = c
        self.ops.append(o)
        self.per_eng[eng].append(o)
        return o

    def dma(self, eng, out, in_, reads, writes, sem):
        if sem == "c0":
            self.nc0 = getattr(self, "nc0", 0) + 1
            sem = "c0_%d" % self.nc0
        return self.op(eng, lambda e: e.dma_start(out=out, in_=in_), reads, writes, dma_sem=sem)

    def barrier(self):
        lasts = []
        for e in ENGS:
            for o in reversed(self.per_eng[e]):
                if o.dma_sem is None and o.fn is not None:
                    lasts.append(o)
                    break
        lastdma = {}
        for o in self.ops:
            if o.dma_sem is not None:
                lastdma[o.dma_sem] = o
        for e in ENGS:
            o = _Op(e, None, (), (), None)
            o.idx = len(self.ops)
            o.eidx = len(self.per_eng[e])
            o.is_bar = True
            o.deps = list(lasts) + list(lastdma.values())
            o.dmawait = {s: 16 * self.dma_counts[s] for s in lastdma}
            self.ops.append(o)
            self.per_eng[e].append(o)

    def finish(self, final_sems=()):
        nc = self.nc
        for o in self.ops:
            keep = []
            for d in o.deps:
                if d.dma_sem is None and d.eng == o.eng and not o.is_bar:
                    if o.eng == "pe" or d.eng == "sp":
                        continue
                    raw = any(k in d.writes for k in o.reads) or any(k in d.writes for k in o.writes)
                    if not raw or (o.eidx - d.eidx) > 3:
                        continue
                keep.append(d)
            o.deps = keep
            for d in keep:
                if d.dma_sem is None:
                    d.flag = True
        cnt = {e: 0 for e in ENGS}
        for o in self.ops:
            if o.dma_sem is None and o.flag:
                cnt[o.eng] += 1
                o.semval = cnt[o.eng]
        dma_names = sorted(self.dma_counts.keys())
        with contextlib.ExitStack() as st:
            esem = {e: st.enter_context(nc.semaphore("sem_" + e)) for e in ENGS}
            dsem = {n: st.enter_context(nc.semaphore("dsem_" + str(n))) for n in dma_names}
            block = st.enter_context(nc.Block())
            handles = {"pe": "tensor", "act": "scalar", "dve": "vector", "pool": "gpsimd", "sp": "sync"}

            def emit(engname):
                def body(e):
                    known = {}
                    for o in self.per_eng[engname]:
                        want = {}
                        for d in o.deps:
                            if d.dma_sem is not None:
                                key = ("d", d.dma_sem)
                                val = o.dmawait[d.dma_sem]
                            else:
                                key = ("e", d.eng)
                                val = d.semval
                            if val > want.get(key, 0):
                                want[key] = val
                        for key, val in want.items():
                            if known.get(key, 0) >= val:
                                continue
                            known[key] = val
                            e.wait_ge(dsem[key[1]] if key[0] == "d" else esem[key[1]], val)
                        if o.fn is None:
                            continue
                        ins = o.fn(e)
                        if o.dma_sem is not None:
                            ins.then_inc(dsem[o.dma_sem], 16)
                        elif o.flag:
                            ins.then_inc(esem[engname], 1)
                    if engname == "sp":
                        for name in final_sems:
                            e.wait_ge(dsem[name], 16 * self.dma_counts[name])
                return body

            for engname in ENGS:
                if self.per_eng[engname] or engname == "sp":
                    getattr(block, handles[engname])(emit(engname))


def build(NJ, dbg=False):
    S = 4096 * NJ
    NB = 8 * NJ
    NT = S // 128
    SO = 512 * NJ
    NTO = SO // 128
    nc = bass.Bass("TRN2", target_bir_lowering=False)
    P = Prog(nc)

    def din(name, shape, dt=F32):
        return nc.dram_tensor(name, list(shape), dt, kind="ExternalInput").ap()

    xa = din("xa", [S, D])
    xo = din("xo", [SO, D])
    posa = din("posa", [128, NT], I32)
    poso = din("poso", [128, NTO], I32)
    invf_d = din("invf", [128, 8])
    w_in = din("w_in", [D, 2048])
    g1_d = din("g1", [128, 8])
    lamv_d = din("lamv", [1, 256])
    subg_d = din("subg", [128, 1])
    lr_nat_d = din("lr_nat", [64, 32]); li_nat_d = din("li_nat", [64, 32]); ls_nat_d = din("ls_nat", [64, 32])
    lr_pair_d = din("lr_pair", [128, 16]); li_pair_d = din("li_pair", [128, 16]); ls_pair_d = din("ls_pair", [128, 16])
    b_re_d = din("b_re_nat", [64, 512]); b_im_d = din("b_im_nat", [64, 512])
    c_re_d = din("c_re_rows", [128, 4, 64]); c_im_d = din("c_im_rows", [128, 4, 64])
    dcol_d = din("dcol", [128, 4])
    w_glu = din("w_glu", [512, 512])
    bglu_d = din("bglu", [128, 4])
    w_out = din("w_out", [D, D])
    g2row_d = din("g2row", [1, D])
    w_gate = din("w_gate", [D, DFF]); w_up = din("w_up", [D, DFF]); w_down = din("w_down", [DFF, D])
    fgrow_d = din("fgrow", [1, D])
    cmask_d = din("cmask", [128, 130])
    cmeta_d = din("cmeta", [128, 64])
    out_d = nc.dram_tensor("out", [SO, D], F32, kind="ExternalOutput").ap()
    dbg_outs = {}

    dkw = dict(kind="ExternalOutput") if dbg else {}
    KT_all = nc.dram_tensor("KT_all", [128, 4, S], BF16, **dkw).ap()
    V_all = nc.dram_tensor("V_all", [4, 128, NT, 128], BF16, **dkw).ap()
    KT_ownd = nc.dram_tensor("KT_ownd", [128, 4, SO], BF16, **dkw).ap()
    V_ownd = nc.dram_tensor("V_ownd", [4, 128, NTO, 128], BF16, **dkw).ap()
    dbg_list = []

    def dump(name, t, keys):
        if not dbg:
            return
        shp = list(t.shape)
        dd = nc.dram_tensor("dbg_" + name, shp, t.dtype, kind="ExternalOutput").ap()
        P.dma("sp", dd, t[:], keys, [("dbg", name)], "dbgw")
        dbg_list.append(name)

    st = contextlib.ExitStack()
    arena = st.enter_context(nc.sbuf_tensor("arena", [128, 207 * 1024], mybir.dt.uint8))
    ABASE = nc.sbuf_base - 207 * 1024
    psum = st.enter_context(nc.psum_tensor("psum", [128, 8, 512], F32))
    base = 0
    KB = 1024

    OFF = {}
    uniq = [0]

    class Alloc:
        def __init__(self, start, end):
            self.cur, self.end = start, end

        def __call__(self, name, shape, dt):
            nbytes = int(np.prod(shape[1:])) * (4 if dt in (F32, I32) else 2)
            nbytes = (nbytes + 31) // 32 * 32
            uniq[0] += 1
            t = nc.alloc_sbuf_tensor_at("%s_%d" % (name, uniq[0]), list(shape), dt, offset=ABASE + self.cur)
            OFF[name] = (self.cur, nbytes)
            self.cur += nbytes
            assert self.cur <= self.end, (name, self.cur, self.end)
            return t

    A0 = Alloc(0, 8 * KB)
    ident_bf = A0("ident_bf", [128, 128], BF16)
    ident_f = A0("ident_f", [128, 128], F32)
    ones_bf = A0("ones_bf", [128, 128], BF16)
    ones_f = A0("ones_f", [128, 128], F32)
    cmask = A0("cmask", [128, 130], F32)
    cmeta = A0("cmeta", [128, 64], F32)
    nMbd = A0("nMbd", [128, 128], F32)
    neghalf = A0("neghalf", [128, 512], F32)
    small = A0("small", [128, 64], F32)
    g1 = A0("g1", [128, 8], F32)
    gcol = A0("gcol", [128, 1], F32)
    dcol = A0("dcolt", [128, 4], F32)
    bglu = A0("bglut", [128, 4], F32)
    invf = A0("invft", [128, 8], F32)

    A1 = Alloc(8 * KB, 116 * KB)
    QT_own = A1("QT_own", [128, 4, 2048], BF16)
    uT_own = A1("uT_own", [128, 4, 2048], BF16)
    TB = A1("TB", [128, 4, 8, 2, 128], BF16)
    CT = A1("CT", [128, 4, 2, 128], BF16)
    W8 = A1("W8", [128, 16, 2, 64], F32)
    PW = A1("PW", [128, 10, 3, 16], F32)
    Et = A1("Et", [128, 2, 32, 16], F32)
    Sp = A1("Sp", [128, 2, 33, 16], F32)
    Sown = A1("Sown", [128, 2, 4, 16], F32)
    Wglu_bf = A1("Wglu_bf", [128, 4, 512], BF16)
    attnT = A1("attnT", [128, 4, 2048], BF16)
    ssmT = A1("ssmT", [128, 4, 2048], BF16)
    T0 = A1.cur
    AS_LO = OFF["attnT"][0]
    AS_HI = OFF["ssmT"][0] + OFF["ssmT"][1]

    ps = lambda b: psum[:, b, :]
    psbf = lambda b: psum[:, b, :].bitcast(BF16)

    def op(eng, method, reads, writes, **kw):
        return P.op(eng, lambda e: getattr(e, method)(**kw), reads, writes)

    def TT(eng, out, in0, in1, o, r, w):
        return op(eng, "tensor_tensor", r, w, out=out, in0=in0, in1=in1, op=o)

    def TSc(eng, out, in0, s1, o0, r, w, s2=None, o1=None):
        kw = dict(out=out, in0=in0, scalar1=s1, scalar2=s2, op0=o0)
        if o1 is not None:
            kw["op1"] = o1
        return op(eng, "tensor_scalar", r, w, **kw)

    def STT(out, in0, scalar, in1, o0, o1, r, w):
        return op("dve", "scalar_tensor_tensor", r, w, out=out, in0=in0, scalar=scalar, in1=in1, op0=o0, op1=o1)

    def ACT(out, in_, func, r, w, **kw):
        return op("act", "activation", r, w, out=out, in_=in_, func=func, **kw)

    def MM(out, lhsT, rhs, r, w, start=True, stop=True, **kw):
        return op("pe", "matmul", r, w, out=out, lhsT=lhsT, rhs=rhs, start=start, stop=stop, **kw)

    P.dma("sp", cmask[:], cmask_d[:, :], [], ["cmask"], "c0")
    P.dma("sp", cmeta[:], cmeta_d[:, :], [], ["cmeta"], "c0")
    P.dma("sp", g1[:], g1_d[:, :], [], ["g1"], "c0")
    P.dma("sp", dcol[:], dcol_d[:, :], [], ["dcol"], "c0")
    P.dma("sp", bglu[:], bglu_d[:, :], [], ["bglu"], "c0")
    P.dma("sp", invf[:], invf_d[:, :], [], ["invf"], "c0")
    P.dma("sp", gcol[:], subg_d[:, :], [], ["gcol"], "c0")
    op("pool", "memset", [], ["ones_bf"], ap=ones_bf[:], constant=1.0)
    op("pool", "memset", [], ["ones_f"], ap=ones_f[:], constant=1.0)
    op("pool", "memset", [], ["neghalf"], ap=neghalf[:], constant=-0.5)
    At = Alloc(T0, 207 * KB)
    Win = At("Win", [128, 8, 2048], BF16)
    ropecs = {}
    for key_, nt_ in (("ra", NT), ("ro", NTO)):
        ropecs[key_] = (At(key_ + "_sn", [128, nt_ * 8], F32), At(key_ + "_cs", [128, nt_ * 8], F32))
    T1 = At.cur
    A = Alloc(T1, 207 * KB)
    AS = Alloc(AS_LO, AS_HI)
    io_t = A("io_t", [128, 128], I32)
    io_f = A("io_f", [128, 128], F32)
    op("pool", "iota", [], ["io_t"], out=io_t[:], pattern=[[1, 128]], base=0, channel_multiplier=-1)
    op("dve", "tensor_copy", ["io_t"], ["io_f"], out=io_f[:], in_=io_t[:])
    TSc("dve", ident_f[:], io_f[:], 0.0, ALU.is_equal, ["io_f"], ["ident_f"])
    op("dve", "tensor_copy", ["ident_f"], ["ident_bf"], out=ident_bf[:], in_=ident_f[:])
    TSc("dve", nMbd[:], cmask[:, 0:128], -1.0, ALU.mult, ["cmask"], ["nMbd"])
    TSc("dve", gcol[:], gcol[:], 1.0 - LAMBDA_INIT, ALU.mult, ["gcol"], ["gcol"])

    lamb = A("lamb", [128, 256], F32)
    lamp = A("lamp", [128, 128], F32)
    P.dma("sp", lamb[:], lamv_d.partition_broadcast(128).rearrange("p o f -> p (o f)"), [], ["lamb"], "c0")
    TT("dve", lamp[:, 0:64], lamb[:, 0:64], lamb[:, 64:128], ALU.mult, ["lamb"], ["lamp"])
    TT("dve", lamp[:, 64:128], lamb[:, 128:192], lamb[:, 192:256], ALU.mult, ["lamb"], ["lamp"])
    op("dve", "tensor_reduce", ["lamp"], ["small"], out=small[:, 2:4],
       in_=lamp[:].rearrange("p (a b) -> p a b", a=2), axis=AX.X, op=ALU.add)
    ACT(small[:, 4:6], small[:, 2:4], AF.Exp, ["small"], ["small"])
    TT("dve", small[:, 6:7], small[:, 5:6], small[:, 4:5], ALU.subtract, ["small"], ["small"])
    TSc("dve", small[:, 0:1], small[:, 6:7], -LAMBDA_INIT, ALU.add, ["small"], ["small"])
    neglam = small[:, 0:1]

    def sincos(theta, shape, key, tmpA, outs=None):
        n = int(np.prod(shape[1:]))
        pcount = shape[0]
        yi = tmpA(key + "_yi", [128, n], I32)
        yf = tmpA(key + "_yf", [128, n], F32)
        r1 = tmpA(key + "_r1", [128, n], F32)
        r2 = tmpA(key + "_r2", [128, n], F32)
        if outs is None:
            sn = tmpA(key + "_sn", [128, n], F32)
            cs = tmpA(key + "_cs", [128, n], F32)
        else:
            sn, cs = outs
        th = theta
        k = key
        sl = slice(0, pcount)
        TSc("dve", yf[sl], th, 1.0 / TWO_PI, ALU.mult, [k + "th"], [k + "yf"])
        op("dve", "tensor_copy", [k + "yf"], [k + "yi"], out=yi[sl], in_=yf[sl])
        op("dve", "tensor_copy", [k + "yi"], [k + "yf"], out=yf[sl], in_=yi[sl])
        STT(r1[sl], yf[sl], -C1, th, ALU.mult, ALU.add, [k + "yf", k + "th"], [k + "r1"])
        STT(r1[sl], yf[sl], -C2, r1[sl], ALU.mult, ALU.add, [k + "yf", k + "r1"], [k + "r1"])
        TSc("dve", r1[sl], r1[sl], PI_SAFE, ALU.min, [k + "r1"], [k + "r1"], s2=-PI_SAFE, o1=ALU.max)
        ACT(sn[sl], r1[sl], AF.Sin, [k + "r1"], [k + "sn"])
        TSc("dve", r2[sl], r1[sl], math.pi / 2, ALU.add, [k + "r1"], [k + "r2"])
        TSc("dve", yf[sl], r2[sl], math.pi, ALU.is_gt, [k + "r2"], [k + "yf"])
        STT(r2[sl], yf[sl], -TWO_PI, r2[sl], ALU.mult, ALU.add, [k + "yf", k + "r2"], [k + "r2"])
        TSc("dve", r2[sl], r2[sl], PI_SAFE, ALU.min, [k + "r2"], [k + "r2"], s2=-PI_SAFE, o1=ALU.max)
        ACT(cs[sl], r2[sl], AF.Sin, [k + "r2"], [k + "cs"])
        return sn, cs

    def rope_tables(pos_d, ntile, key):
        Ar = Alloc(T1 + 28 * KB, 207 * KB)
        posi = Ar("rp_pi", [128, ntile], I32)
        posf = Ar("rp_pf", [128, ntile], F32)
        th = Ar("rp_th", [128, ntile * 8], F32)
        P.dma("sp", posi[:], pos_d[:, :], [], [key + "pi"], "c0")
        op("dve", "tensor_copy", [key + "pi"], [key + "pf"], out=posf[:], in_=posi[:])
        TT("dve", th[:].rearrange("p (t f) -> p t f", f=8),
           posf[:].unsqueeze(2).broadcast_to([128, ntile, 8]),
           invf[:].unsqueeze(1).broadcast_to([128, ntile, 8]), ALU.mult, [key + "pf", "invf"], [key + "th"])
        sn, cs = sincos(th[:], [128, ntile * 8], key, Ar, outs=ropecs[key])
        return (cs[:].rearrange("p (t f) -> p t f", f=8), sn[:].rearrange("p (t f) -> p t f", f=8), key)

    ropeA = rope_tables(posa, NT, "ra")
    P.barrier()
    ropeO = rope_tables(poso, NTO, "ro")
    P.barrier()

    def a_chain(lr_d, li_d, ls_d, np_, nf, key):
        sl = slice(0, np_)
        t = {}
        for nm in ("lr", "li", "ls", "step", "e1", "mag", "th", "are", "aim", "den", "nr", "t1", "t2", "cre", "cim"):
            t[nm] = A(key + nm, [128, nf], F32)
        P.dma("sp", t["lr"][sl], lr_d[:, :], [], [key + "lr"], "c0")
        P.dma("sp", t["li"][sl], li_d[:, :], [], [key + "li"], "c0")
        P.dma("sp", t["ls"][sl], ls_d[:, :], [], [key + "ls"], "c0")
        k = key
        ACT(t["step"][sl], t["ls"][sl], AF.Exp, [k + "ls"], [k + "step"])
        TSc("dve", t["lr"][sl], t["lr"][sl], -1e-4, ALU.min, [k + "lr"], [k + "lr"])
        TT("dve", t["e1"][sl], t["lr"][sl], t["step"][sl], ALU.mult, [k + "lr", k + "step"], [k + "e1"])
        ACT(t["mag"][sl], t["e1"][sl], AF.Exp, [k + "e1"], [k + "mag"])
        TT("dve", t["th"][sl], t["li"][sl], t["step"][sl], ALU.mult, [k + "li", k + "step"], [k + "th" + "th"])
        sn, cs = sincos(t["th"][sl], [np_, nf], k + "th", A)
        TT("dve", t["are"][sl], t["mag"][sl], cs[sl], ALU.mult, [k + "mag", k + "thcs"], [k + "are"])
        TT("dve", t["aim"][sl], t["mag"][sl], sn[sl], ALU.mult, [k + "mag", k + "thsn"], [k + "aim"])
        TT("dve", t["t1"][sl], t["lr"][sl], t["lr"][sl], ALU.mult, [k + "lr"], [k + "t1"])
        TT("dve", t["t2"][sl], t["li"][sl], t["li"][sl], ALU.mult, [k + "li"], [k + "t2"])
        TT("dve", t["den"][sl], t["t1"][sl], t["t2"][sl], ALU.add, [k + "t1", k + "t2"], [k + "den"])
        op("dve", "reciprocal", [k + "den"], [k + "den"], out=t["den"][sl], in_=t["den"][sl])
        TSc("dve", t["nr"][sl], t["are"][sl], -1.0, ALU.add, [k + "are"], [k + "nr"])
        TT("dve", t["t1"][sl], t["nr"][sl], t["lr"][sl], ALU.mult, [k + "nr", k + "lr"], [k + "t1"])
        TT("dve", t["t2"][sl], t["aim"][sl], t["li"][sl], ALU.mult, [k + "aim", k + "li"], [k + "t2"])
        TT("dve", t["cre"][sl], t["t1"][sl], t["t2"][sl], ALU.add, [k + "t1", k + "t2"], [k + "cre"])
        TT("dve", t["cre"][sl], t["cre"][sl], t["den"][sl], ALU.mult, [k + "cre", k + "den"], [k + "cre"])
        TT("dve", t["t1"][sl], t["aim"][sl], t["lr"][sl], ALU.mult, [k + "aim", k + "lr"], [k + "t1"])
        TT("dve", t["t2"][sl], t["nr"][sl], t["li"][sl], ALU.mult, [k + "nr", k + "li"], [k + "t2"])
        TT("dve", t["cim"][sl], t["t1"][sl], t["t2"][sl], ALU.subtract, [k + "t1", k + "t2"], [k + "cim"])
        TT("dve", t["cim"][sl], t["cim"][sl], t["den"][sl], ALU.mult, [k + "cim", k + "den"], [k + "cim"])
        return t

    tmpc1 = AS("tmpc1", [128, 2048], F32)
    tmpc2 = AS("tmpc2", [128, 2048], F32)

    def cmul(o_re, o_im, a_re, a_im, b_re, b_im, shape, rk, wk):
        n = int(np.prod(shape[1:]))
        sl = slice(0, shape[0])
        x1 = tmpc1[sl, 0:n]
        x2 = tmpc2[sl, 0:n]
        if len(shape) == 3:
            x1 = x1.rearrange("p (a b) -> p a b", a=shape[1])
            x2 = x2.rearrange("p (a b) -> p a b", a=shape[1])
        TT("dve", x1, a_re, b_re, ALU.mult, rk, ["tmpc1"])
        TT("dve", x2, a_im, b_im, ALU.mult, rk, ["tmpc2"])
        TT("dve", o_re, x1, x2, ALU.subtract, ["tmpc1", "tmpc2"], wk)
        TT("dve", x1, a_re, b_im, ALU.mult, rk, ["tmpc1"])
        TT("dve", x2, a_im, b_re, ALU.mult, rk, ["tmpc2"])
        TT("dve", o_im, x1, x2, ALU.add, ["tmpc1", "tmpc2"], wk)

    cp = a_chain(lr_pair_d, li_pair_d, ls_pair_d, 128, 16, "cp")
    op("dve", "tensor_copy", ["cpare"], ["PW"], out=PW[:, 0, 0, :], in_=cp["are"][:])
    op("dve", "tensor_copy", ["cpaim"], ["PW"], out=PW[:, 0, 1, :], in_=cp["aim"][:])
    for k in range(1, 10):
        cmul(PW[:, k, 0, :], PW[:, k, 1, :], PW[:, k - 1, 0, :], PW[:, k - 1, 1, :],
             PW[:, k - 1, 0, :], PW[:, k - 1, 1, :], [128, 16], ["PW"], ["PW"])
    TSc("dve", PW[:, :, 2, :], PW[:, :, 1, :], -1.0, ALU.mult, ["PW"], ["PW"])
    op("pool", "memset", [], ["W8"], ap=W8[:, :, 0, :], constant=1.0)
    op("pool", "memset", ["W8"], ["W8"], ap=W8[:, :, 1, :], constant=0.0)
    op("dve", "tensor_copy", ["PW", "W8"], ["W8"], out=W8[:, :, 0, 62:63], in_=PW[:, 3, 0, :].unsqueeze(2))
    op("dve", "tensor_copy", ["PW", "W8"], ["W8"], out=W8[:, :, 1, 62:63], in_=PW[:, 3, 1, :].unsqueeze(2))
    lo = 62
    for k in range(4, 9):
        n = 64 - lo
        nlo = lo - n
        bre = PW[:, k, 0, :].unsqueeze(2).broadcast_to([128, 16, n])
        bim = PW[:, k, 1, :].unsqueeze(2).broadcast_to([128, 16, n])
        cmul(W8[:, :, 0, nlo:lo], W8[:, :, 1, nlo:lo], W8[:, :, 0, lo:64], W8[:, :, 1, lo:64], bre, bim,
             [128, 16, n], ["W8", "PW"], ["W8"])
        lo = nlo
    assert lo == 0

    cn = a_chain(lr_nat_d, li_nat_d, ls_nat_d, 64, 32, "cn")
    Bn = [AS("Bn%d" % i, [128, 512], F32) for i in range(2)]
    Bb = [AS("Bb%d" % i, [128, 512], F32) for i in range(2)]
    Wv = [AS("Wv%d" % i, [128, 512], F32) for i in range(2)]
    apn = [[A("apn%d_%d" % (v, i), [128, 32], F32) for i in range(2)] for v in range(8)]
    P.dma("sp", Bn[0][0:64], b_re_d[:, :], [], ["Bn"], "c0")
    P.dma("sp", Bn[1][0:64], b_im_d[:, :], [], ["Bn"], "c0")
    v3 = lambda t: t[0:64, :].rearrange("p (g h) -> p g h", h=16)
    b3 = lambda t: t[0:64, :].unsqueeze(2).broadcast_to([64, 32, 16])
    cmul(v3(Bb[0]), v3(Bb[1]), b3(cn["cre"]), b3(cn["cim"]), v3(Bn[0]), v3(Bn[1]), [64, 32, 16],
         ["Bn", "cncre", "cncim"], ["Bb"])
    op("pool", "memset", [], ["apn"], ap=apn[0][0][0:64], constant=1.0)
    op("pool", "memset", [], ["apn"], ap=apn[0][1][0:64], constant=0.0)
    for v in range(1, 8):
        cmul(apn[v][0][0:64], apn[v][1][0:64], apn[v - 1][0][0:64], apn[v - 1][1][0:64],
             cn["are"][0:64], cn["aim"][0:64], [64, 32], ["apn", "cnare", "cnaim"], ["apn"])
    tb_bank = 0
    for v in range(8):
        if v == 0:
            src = Bb
        else:
            cmul(v3(Wv[0]), v3(Wv[1]), b3(apn[v][0]), b3(apn[v][1]), v3(Bb[0]), v3(Bb[1]), [64, 32, 16],
                 ["apn", "Bb"], ["Wv"])
            src = Wv
        for ri in range(2):
            for t in range(4):
                b = tb_bank % 8
                tb_bank += 1
                op("pe", "transpose", ["Wv", "Bb", "ident_f"], [("ps", b)], out=ps(b)[:, 0:64],
                   in_=src[ri][0:64, t * 128:(t + 1) * 128], identity=ident_f[0:64, 0:64])
                for e_ in range(2):
                    TSc("dve", TB[:, t, v, ri, 64 * e_:64 * e_ + 64], ps(b)[:, 0:64], cmask[:, 128 + e_:129 + e_],
                        ALU.mult, [("ps", b), "cmask"], ["TB"])
    Cd = [A("Cd%d" % i, [128, 4, 128], F32) for i in range(2)]
    for ri, cd in enumerate((c_re_d, c_im_d)):
        P.dma("sp", Cd[ri][:, :, 0:64], cd[:, :, :], [], ["Cd"], "c0")
        P.dma("sp", Cd[ri][:, :, 64:128], cd[:, :, :], [], ["Cd"], "c0")
        for t in range(4):
            b = tb_bank % 8
            tb_bank += 1
            op("pe", "transpose", ["Cd", "ident_f"], [("ps", b)], out=ps(b)[:, 0:128], in_=Cd[ri][:, t, :],
               identity=ident_f[:])
            TT("dve", CT[:, t, ri, :], ps(b)[:, 0:128], cmask[:, 0:128] if ri == 0 else nMbd[:], ALU.mult,
               [("ps", b), "cmask", "nMbd"], ["CT"])

    wst = A("wst", [128, 2048], F32)
    for kt in range(8):
        P.dma("sp", wst[:], w_in[kt * 128:(kt + 1) * 128, :], [], ["wst"], "wst")
        TSc("dve", Win[:, kt, :], wst[:], g1[:, kt:kt + 1], ALU.mult, ["wst", "g1"], [("Win", kt)])
    for ct in range(4):
        P.dma("pool", Wglu_bf[:, ct, :], w_glu[ct * 128:(ct + 1) * 128, :], [], ["Wglu"], "wglu")
    WinK = [("Win", kt) for kt in range(8)]
    assert A.cur <= T1 + 28 * KB, (A.cur, T1)

    P.barrier()
    A = Alloc(T1, 207 * KB)
    AS = Alloc(AS_LO, AS_HI)
    xst = A("xst", [128, 4, D], F32)
    xb = A("xb", [128, 4, D], BF16)
    hnT = A("hnT", [128, 8, 512], BF16)
    Kst = A("Kst", [128, 4, 512], BF16)
    Vst = A("Vst", [128, 4, 4, 128], BF16)
    Qst = A("Qst", [128, 4, 512], BF16)
    KTst = A("KTst", [128, 4, 512], BF16)
    rt = [A("rt%d" % i, [128, 64], F32) for i in range(4)]
    ssq = A("ssq", [128, 8], F32)
    uTh = AS("uTh", [128, 4, 2048], BF16)
    junk = AS("junk", [128, D], BF16)
    sred = [AS("sred%d" % i, [128, 256], F32) for i in range(4)]

    bankrr = [0]

    def nbank(allowed=(0, 1, 2, 3, 4, 5, 6, 7)):
        b = allowed[bankrr[0] % len(allowed)]
        bankrr[0] += 1
        return b

    def rope_evac(psb, dst, rope, ti, dk):
        cs, sn, rk = rope
        p3 = ps(psb).rearrange("p (m d) -> p m d", d=64)
        d3 = dst.rearrange("p (m d) -> p m d", d=64)
        cb = cs[:, ti, :].unsqueeze(1).broadcast_to([128, 8, 8])
        sb = sn[:, ti, :].unsqueeze(1).broadcast_to([128, 8, 8])
        rd = [("ps", psb), rk + "cs", rk + "sn"]
        r3 = [r[:].rearrange("p (m f) -> p m f", f=8) for r in rt]
        op("dve", "tensor_copy", [("ps", psb)], [dk], out=dst, in_=ps(psb))
        TT("dve", r3[0], p3[:, :, 0:8], cb, ALU.mult, rd, ["rt0"])
        TT("dve", r3[1], p3[:, :, 8:16], sb, ALU.mult, rd, ["rt1"])
        TT("dve", r3[2], p3[:, :, 8:16], cb, ALU.mult, rd, ["rt2"])
        TT("dve", r3[3], p3[:, :, 0:8], sb, ALU.mult, rd, ["rt3"])
        TT("dve", d3[:, :, 0:8], r3[0], r3[1], ALU.subtract, ["rt0", "rt1", dk], [dk])
        TT("dve", d3[:, :, 8:16], r3[2], r3[3], ALU.add, ["rt2", "rt3", dk], [dk])

    def stream_block(x_rows, rope, tile0, own, ublk_dst, ukey, kt_dst, v_dst, qdst=None):
        P.dma("sp", xst[:], x_rows.rearrange("(t p) d -> p t d", p=128), [], ["xst"], "xst")
        for t in range(4):
            ACT(junk[:], xst[:, t, :], AF.Square, ["xst"], ["junk", "ssq"], accum_out=ssq[:, t:t + 1])
        TSc("dve", ssq[:, 4:8], ssq[:, 0:4], 1.0 / D, ALU.mult, ["ssq"], ["ssq2"], s2=EPS, o1=ALU.add)
        TT("pool", ssq[:, 4:8], ssq[:, 4:8], neghalf[:, 0:4], ALU.pow, ["ssq2", "neghalf"], ["ssq2"])
        for t in range(4):
            TSc("pool", xb[:, t, :], xst[:, t, :], ssq[:, 4 + t:5 + t], ALU.mult, ["xst", "ssq2"], [("xb", t)])
        for kt in range(8):
            b = nbank((0, 1))
            for t in range(4):
                op("pe", "transpose", [("xb", t), "ident_bf"], [("ps", b)], out=psbf(b)[:, t * 128:(t + 1) * 128],
                   in_=xb[:, t, kt * 128:(kt + 1) * 128], identity=ident_bf[:])
            if kt % 2 == 0:
                op("act", "copy", [("ps", b)], [("hnT", kt)], out=hnT[:, kt, :], in_=psbf(b)[:, 0:512])
            else:
                op("dve", "tensor_copy", [("ps", b)], [("hnT", kt)], out=hnT[:, kt, :], in_=psbf(b)[:, 0:512])
        hk = [("hnT", kt) for kt in range(8)]
        for t in range(4):
            bk, bv = nbank((2, 3, 4, 5)), nbank((2, 3, 4, 5))
            for kt in range(8):
                lt = hnT[:, kt, t * 128:(t + 1) * 128]
                MM(ps(bk), lt, Win[:, kt, 512:1024], hk + WinK, [("ps", bk)], start=kt == 0, stop=kt == 7)
                MM(ps(bv), lt, Win[:, kt, 1024:1536], hk + WinK, [("ps", bv)], start=kt == 0, stop=kt == 7)
            if own:
                bq = nbank((2, 3, 4, 5))
                for kt in range(8):
                    MM(ps(bq), hnT[:, kt, t * 128:(t + 1) * 128], Win[:, kt, 0:512], hk + WinK, [("ps", bq)],
                       start=kt == 0, stop=kt == 7)
            op("act", "copy", [("ps", bv)], [("Vst", t)], out=Vst[:, :, t, :],
               in_=ps(bv).rearrange("p (h d) -> p h d", h=4))
            rope_evac(bk, Kst[:, t, :], rope, tile0 + t, ("Kst", t))
            if own:
                rope_evac(bq, Qst[:, t, :], rope, tile0 + t, ("Qst", t))
        for ct in range(4):
            b = nbank((6, 7))
            for kt in range(8):
                MM(ps(b), Win[:, kt, 1536 + ct * 128:1536 + (ct + 1) * 128], hnT[:, kt, :], hk + WinK, [("ps", b)],
                   start=kt == 0, stop=kt == 7)
            op("act", "copy", [("ps", b)], [ukey], out=ublk_dst[:, ct, :], in_=ps(b))
        for (srcst, skey, dstT, dkey) in ([(Kst, "Kst", None, None)] + ([(Qst, "Qst", qdst, "QT_own")] if own else [])):
            for ft in range(4):
                b = nbank((0, 1))
                for t in range(4):
                    op("pe", "transpose", [(skey, t), "ident_bf"], [("ps", b)],
                       out=psbf(b)[:, t * 128:(t + 1) * 128], in_=srcst[:, t, ft * 128:(ft + 1) * 128],
                       identity=ident_bf[:])
                if dstT is None:
                    op("dve", "tensor_copy", [("ps", b)], ["KTst"], out=KTst[:, ft, :], in_=psbf(b)[:, 0:512])
                else:
                    op("dve", "tensor_copy", [("ps", b)], [dkey], out=dstT[:, ft, :], in_=psbf(b)[:, 0:512])
        P.dma("sp", kt_dst[0], KTst[:], ["KTst"], [kt_dst[1]], "ktw")
        P.dma("sp", v_dst[0], Vst[:], [("Vst", t) for t in range(4)], [v_dst[1]], "vw")

    def ssm_prefix_half(hb):
        for q in range(16):
            p0 = 32 * (q % 4)
            ct = q // 4
            bb = [nbank((2, 3, 4, 5)), nbank((2, 3, 4, 5))]
            for ri in range(2):
                for s_ in range(8):
                    MM(ps(bb[ri])[:, 0:256], TB[p0:p0 + 32, ct, 7 - s_, ri, :], uTh[p0:p0 + 32, ct, s_:2048:8],
                       ["TB", "uTh"], [("ps", bb[ri])], start=s_ == 0, stop=s_ == 7, tile_position=(p0, 0))
            Lr = ps(bb[0])[:, 0:256].rearrange("p (b c) -> p b c", c=64)
            Li = ps(bb[1])[:, 0:256].rearrange("p (b c) -> p b c", c=64)
            Wr = W8[:, q, 0, :].unsqueeze(1).broadcast_to([128, 4, 64])
            Wi = W8[:, q, 1, :].unsqueeze(1).broadcast_to([128, 4, 64])
            s3 = [s[:].rearrange("p (b c) -> p b c", c=64) for s in sred]
            rd = [("ps", bb[0]), ("ps", bb[1]), "W8"]
            TT("dve", s3[0], Lr, Wr, ALU.mult, rd, ["sred0"])
            TT("dve", s3[1], Li, Wi, ALU.mult, rd, ["sred1"])
            TT("dve", s3[2], Lr, Wi, ALU.mult, rd, ["sred2"])
            TT("dve", s3[3], Li, Wr, ALU.mult, rd, ["sred3"])
            TT("dve", s3[0], s3[0], s3[1], ALU.subtract, ["sred0", "sred1"], ["sred0"])
            TT("dve", s3[2], s3[2], s3[3], ALU.add, ["sred2", "sred3"], ["sred2"])
            op("dve", "tensor_reduce", ["sred0"], ["Et"], out=Et[:, 0, 4 * hb:4 * hb + 4, q], in_=s3[0], axis=AX.X,
               op=ALU.add)
            op("dve", "tensor_reduce", ["sred2"], ["Et"], out=Et[:, 1, 4 * hb:4 * hb + 4, q], in_=s3[2], axis=AX.X,
               op=ALU.add)

    for j in range(NJ):
        stream_block(xo[j * 512:(j + 1) * 512, :], ropeO, 4 * j, True,
                     uT_own[:, :, j * 512:(j + 1) * 512], "uT_own",
                     (KT_ownd[:, :, j * 512:(j + 1) * 512], ("KTo", j)),
                     (V_ownd[:, :, 4 * j:4 * j + 4, :].rearrange("h p t d -> p h t d"), ("Vo", j)),
                     qdst=QT_own[:, :, j * 512:(j + 1) * 512])
    for i in range(NB):
        stream_block(xa[i * 512:(i + 1) * 512, :], ropeA, 4 * i, False,
                     uTh[:, :, (i % 4) * 512:(i % 4 + 1) * 512], "uTh",
                     (KT_all[:, :, i * 512:(i + 1) * 512], ("KTa", i)),
                     (V_all[:, :, 4 * i:4 * i + 4, :].rearrange("h p t d -> p h t d"), ("Va", i)))
        if i % 4 == 3:
            ssm_prefix_half(i // 4)

    op("pool", "memset", [], ["Sp"], ap=Sp[:, :, 0, :], constant=0.0)
    sr = [r[:, 0:16] for r in rt]
    for b in range(NB - 1):
        Are, Aim = PW[:, 9, 0, :], PW[:, 9, 1, :]
        TT("dve", sr[0], Sp[:, 0, b, :], Are, ALU.mult, ["Sp", "PW"], ["rt0"])
        TT("dve", sr[1], Sp[:, 1, b, :], Aim, ALU.mult, ["Sp", "PW"], ["rt1"])
        TT("dve", sr[2], Sp[:, 0, b, :], Aim, ALU.mult, ["Sp", "PW"], ["rt2"])
        TT("dve", sr[3], Sp[:, 1, b, :], Are, ALU.mult, ["Sp", "PW"], ["rt3"])
        TT("dve", sr[0], sr[0], sr[1], ALU.subtract, ["rt0", "rt1"], ["rt0"])
        TT("dve", sr[2], sr[2], sr[3], ALU.add, ["rt2", "rt3"], ["rt2"])
        TT("dve", Sp[:, 0, b + 1, :], sr[0], Et[:, 0, b, :], ALU.add, ["rt0", "Et"], ["Sp"])
        TT("dve", Sp[:, 1, b + 1, :], sr[2], Et[:, 1, b, :], ALU.add, ["rt2", "Et"], ["Sp"])
    op("pool", "memset", [], ["Sown"], ap=Sown[:], constant=0.0)
    for j in range(NJ):
        for ri in range(2):
            for i in range(8):
                STT(Sown[:, ri, j, :], Sp[:, ri, 8 * j + i, :], cmeta[:, 32 + i:33 + i], Sown[:, ri, j, :],
                    ALU.mult, ALU.add, ["Sp", "cmeta", "Sown"], ["Sown"])

    P.barrier()
    dump("QT_own", QT_own, []); dump("uT_own", uT_own, []); dump("Et", Et, []); dump("Sp", Sp, []); dump("Sown", Sown, [])
    dump("TB", TB, []); dump("CT", CT, []); dump("W8", W8, []); dump("PW", PW, []); dump("small", small, [])
    P.barrier()

    A = Alloc(T0, 207 * KB)
    NKB = 4
    KTc = [A("KTc%d" % i, [128, 1024], BF16) for i in range(NKB)]
    Vc = [A("Vc%d" % i, [128, 8, 128], BF16) for i in range(NKB)]
    PT = [A("PT%d" % i, [128, 2, 512], BF16) for i in range(3)]
    maskD = A("maskD", [128, 4, 512], BF16)
    mio = A("mio", [128, 512], I32)
    miof = A("miof", [128, 512], F32)
    rl = A("rl", [128, 512], F32)
    Bsb = [A("Bsb%d" % i, [128, 512], F32) for i in range(2)]
    tm = [A("tm%d" % i, [128, 512], F32) for i in range(2)]
    Dt = A("Dt", [128, 512], F32)
    sq = A("sq", [128, 512], F32)
    msr = A("msr", [128, 512], F32)
    op("pool", "iota", [], ["mio"], out=mio[:], pattern=[[1, 512]], base=0, channel_multiplier=-1)
    op("dve", "tensor_copy", ["mio"], ["miof"], out=miof[:], in_=mio[:])
    for d_ in range(4):
        TSc("dve", maskD[:, d_, :], miof[:], float(128 * d_), ALU.is_ge, ["miof"], ["maskD"])

    kvrr = [0]
    ptrr = [0]
    sbank = [0]
    for j in range(NJ):
        nglob = 32 * j + 32
        for h in range(4):
            chunks = [("g", k0) for k0 in range(0, nglob, 8)] + [("d", 0)]
            ntile_total = nglob + 4
            tcount = 0
            for (kind, k0) in chunks:
                sl = kvrr[0] % NKB
                kvrr[0] += 1
                ntc = 8 if kind == "g" else 4
                if kind == "g":
                    blks = sorted({(k0 + x) // 4 for x in range(8)})
                    P.dma("sp", KTc[sl][:], KT_all[:, h, k0 * 128:(k0 + 8) * 128],
                          [("KTa", b_) for b_ in blks], [("KTc", sl)], "ktc%d" % sl)
                    P.dma("pool", Vc[sl][:], V_all[h, :, k0:k0 + 8, :],
                          [("Va", b_) for b_ in blks], [("Vc", sl)], "vc%d" % sl)
                else:
                    P.dma("sp", KTc[sl][:, 0:512], KT_ownd[:, h, j * 512:(j + 1) * 512],
                          [("KTo", j)], [("KTc", sl)], "ktc%d" % sl)
                    P.dma("pool", Vc[sl][:, 0:4, :], V_ownd[h, :, 4 * j:4 * j + 4, :],
                          [("Vo", j)], [("Vc", sl)], "vc%d" % sl)
                for k in range(ntc):
                    sb_ = (0, 2)[sbank[0] % 2]
                    sbank[0] += 1
                    pt = ptrr[0] % 3
                    ptrr[0] += 1
                    for m in range(2):
                        MM(ps(sb_ + m), KTc[sl][64 * m:64 * m + 64, k * 128:(k + 1) * 128],
                           QT_own[64 * m:64 * m + 64, h, j * 512:(j + 1) * 512], [("KTc", sl), "QT_own"],
                           [("ps", sb_ + m)], tile_position=(64 * m, 0))
                    bias = 0.0
                    if kind == "g" and k0 + k >= 32 * j:
                        r_ = k0 + k - 32 * j
                        bias = cmeta[:, r_:r_ + 1]
                    ACT(PT[pt][:], psum[:, sb_:sb_ + 2, :], AF.Exp, [("ps", sb_), ("ps", sb_ + 1), "cmeta"],
                        [("PT", pt)], scale=0.125, bias=bias)
                    if kind == "d":
                        TT("dve", PT[pt][:], PT[pt][:], maskD[:, k, :].unsqueeze(1).broadcast_to([128, 2, 512]),
                           ALU.mult, [("PT", pt), "maskD"], [("PT", pt)])
                    first = tcount == 0
                    last = tcount == ntile_total - 1
                    for m in range(2):
                        MM(ps(4 + m), Vc[sl][:, k, :], PT[pt][:, m, :], [("Vc", sl), ("PT", pt)], [("ps", 4 + m)],
                           start=first, stop=last)
                    for m in range(2):
                        MM(ps(6)[32 * m:32 * m + 32, :], ones_bf[:, 0:32], PT[pt][:, m, :], [("PT", pt), "ones_bf"],
                           [("ps", 6)], start=first, stop=last, tile_position=(0, 32 * m))
                    tcount += 1
            op("dve", "reciprocal", [("ps", 6)], ["rl"], out=rl[0:64, :], in_=ps(6)[0:64, :])
            for m in range(2):
                MM(ps(7), ones_f[32 * m:32 * m + 1, :], rl[32 * m:32 * m + 1, :], ["rl", "ones_f"], [("ps", 7)],
                   tile_position=(32 * m, 0))
                op("act", "copy", [("ps", 7)], [("Bsb", m)], out=Bsb[m][:], in_=ps(7))
                TT("dve", tm[m][:], ps(4 + m), Bsb[m][:], ALU.mult, [("ps", 4 + m), ("Bsb", m)], [("tm", m)])
            STT(Dt[:], tm[1][:], neglam, tm[0][:], ALU.mult, ALU.add, [("tm", 0), ("tm", 1), "small"], ["Dt"])
            TT("pool", sq[:], Dt[:], Dt[:], ALU.mult, ["Dt"], ["sq"])
            MM(ps(7), ones_f[:], sq[:], ["sq", "ones_f"], [("ps", 7)])
            TSc("dve", msr[:], ps(7), 1.0 / 128, ALU.mult, [("ps", 7)], ["msr"], s2=EPS, o1=ALU.add)
            TT("pool", msr[:], msr[:], neghalf[:], ALU.pow, ["msr", "neghalf"], ["msr"])
            TT("dve", Dt[:], Dt[:], msr[:], ALU.mult, ["Dt", "msr"], ["Dt"])
            TSc("dve", attnT[:, h, j * 512:(j + 1) * 512], Dt[:], gcol[:, 0:1], ALU.mult, ["Dt", "gcol"], ["attnT"])

    P.barrier()

    A = Alloc(T0, 207 * KB)
    X = A("X", [128, 8, 2, 512], F32)
    Xb = A("Xb", [128, 8, 2, 512], BF16)
    yv = A("yv", [128, 4, 512], F32)
    y2 = A("y2", [128, 512], F32)
    gT = A("gT", [128, 4, 512], BF16)
    sg = A("sg", [128, 512], F32)
    aS = A("aS", [128, 2, 16], F32)
    tmpc1 = A("tmpc1b", [128, 64], F32)
    tmpc2 = A("tmpc2b", [128, 64], F32)
    for j in range(NJ):
        c0 = j * 512
        cmul(aS[:, 0, :], aS[:, 1, :], PW[:, 0, 0, :], PW[:, 0, 1, :], Sown[:, 0, j, :], Sown[:, 1, j, :], [128, 16],
             ["PW", "Sown"], ["aS"])
        for half in range(2):
            for qq in range(8):
                q = half * 8 + qq
                p0 = 32 * (q % 4)
                ct = q // 4
                for ri in range(2):
                    b = nbank((0, 1, 2, 3))
                    MM(ps(b), TB[p0:p0 + 32, ct, 0, ri, :], uT_own[p0:p0 + 32, ct, c0:c0 + 512], ["TB", "uT_own"],
                       [("ps", b)], tile_position=(p0, 0))
                    if (qq + ri) % 2 == 0:
                        op("act", "copy", [("ps", b)], [("X", qq)], out=X[:, qq, ri, :], in_=ps(b))
                    else:
                        op("dve", "tensor_copy", [("ps", b)], [("X", qq)], out=X[:, qq, ri, :], in_=ps(b))
            for ri in range(2):
                TT("dve", X[:, :, ri, 0:1], X[:, :, ri, 0:1], aS[:, ri, half * 8:half * 8 + 8].unsqueeze(2), ALU.add,
                   [("X", qq) for qq in range(8)] + ["aS"], [("X", qq) for qq in range(8)])
            for qq in range(8):
                q = half * 8 + qq
                xk = [("X", qq)]

                def lvl(k, lo_sl, hi_sl):
                    ar = PW[:, k, 0, q:q + 1]
                    ai = PW[:, k, 1, q:q + 1]
                    nai = PW[:, k, 2, q:q + 1]
                    xr, xi = X[:, qq, 0, :], X[:, qq, 1, :]
                    STT(xr[:, hi_sl], xr[:, lo_sl], ar, xr[:, hi_sl], ALU.mult, ALU.add, xk + ["PW"], xk)
                    STT(xr[:, hi_sl], xi[:, lo_sl], nai, xr[:, hi_sl], ALU.mult, ALU.add, xk + ["PW"], xk)
                    STT(xi[:, hi_sl], xi[:, lo_sl], ar, xi[:, hi_sl], ALU.mult, ALU.add, xk + ["PW"], xk)
                    STT(xi[:, hi_sl], xr[:, lo_sl], ai, xi[:, hi_sl], ALU.mult, ALU.add, xk + ["PW"], xk)

                for k in range(9):
                    st2 = 2 << k
                    lvl(k, slice((1 << k) - 1, 512, st2), slice(st2 - 1, 512, st2))
                for k in range(7, -1, -1):
                    st2 = 2 << k
                    lvl(k, slice(st2 - 1, 512 - (1 << k), st2), slice(st2 + (1 << k) - 1, 512, st2))
                op("pool", "tensor_copy", xk, [("Xb", qq)], out=Xb[:, qq, :, :], in_=X[:, qq, :, :])
            for cl in range(2):
                ct = half * 2 + cl
                b = nbank((4, 5))
                for ql in range(4):
                    qq = cl * 4 + ql
                    for ri in range(2):
                        MM(ps(b)[32 * ql:32 * ql + 32, :], CT[:, ct, ri, 32 * ql:32 * ql + 32], Xb[:, qq, ri, :],
                           ["CT", ("Xb", qq)], [("ps", b)], start=ri == 0, stop=ri == 1, tile_position=(0, 32 * ql))
                STT(yv[:, ct, :], uT_own[:, ct, c0:c0 + 512], dcol[:, ct:ct + 1], ps(b), ALU.mult, ALU.add,
                    ["uT_own", "dcol", ("ps", b)], [("yv", ct)])
        for ct in range(4):
            TT("pool", y2[:], yv[:, ct, :], yv[:, ct, :], ALU.mult, [("yv", ct)], ["y2"])
            TSc("dve", y2[:], y2[:], GELU_K2, ALU.mult, ["y2"], ["y2"], s2=GELU_K1, o1=ALU.add)
            TT("dve", y2[:], y2[:], yv[:, ct, :], ALU.mult, ["y2", ("yv", ct)], ["y2"])
            ACT(sg[:], y2[:], AF.Sigmoid, ["y2"], ["sg"])
            TT("dve", gT[:, ct, :], sg[:], yv[:, ct, :], ALU.mult, ["sg", ("yv", ct)], [("gT", ct)])
        for ot in range(4):
            b = nbank((6, 7))
            for ct in range(4):
                MM(ps(b), Wglu_bf[:, ct, ot * 128:(ot + 1) * 128], gT[:, ct, :], ["Wglu", ("gT", ct)], [("ps", b)],
                   start=ct == 0, stop=ct == 3)
            ACT(sg[:], ps(b), AF.Sigmoid, [("ps", b), "bglu"], ["sg"], bias=bglu[:, ot:ot + 1])
            TT("dve", ssmT[:, ot, c0:c0 + 512], sg[:], gT[:, ot, :], ALU.mult, ["sg", ("gT", ot)], ["ssmT"])

    P.barrier()
    dump("attnT", attnT, []); dump("ssmT", ssmT, [])
    P.barrier()

    A = Alloc(T0, 207 * KB)
    h_sb = A("h_sb", [128, 16, D], F32)
    H_END = A.cur
    xs2 = [A("xs2_%d" % i, [128, D], F32) for i in range(2)]
    g2row = A("g2row", [128, D], F32)
    hb16 = A("hb16", [128, D], BF16)
    ss2 = A("ss2", [128, 4], F32)
    Ab = Alloc(8 * KB, OFF["TB"][0])
    h2nT = Ab("h2nT", [128, 8, 2048], BF16)
    Aw = Alloc(OFF["TB"][0], AS_LO)
    Wout_bf = Aw("Wout_bf", [128, 8, D], BF16)
    P.dma("sp", g2row[:], g2row_d.partition_broadcast(128).rearrange("p o f -> p (o f)"), [], ["g2row"], "c1")
    for kt in range(8):
        P.dma("pool", Wout_bf[:, kt, :], w_out[kt * 128:(kt + 1) * 128, :], [], ["Wout"], "wout")
    for tt in range(NTO):
        xb_ = xs2[tt % 2]
        P.dma("sp", xb_[:], xo[tt * 128:(tt + 1) * 128, :], [], [("xs2", tt % 2)], "xs2_%d" % (tt % 2))
        for nh in range(2):
            b = nbank((0, 1, 2, 3))
            for ft in range(8):
                src = attnT if ft < 4 else ssmT
                MM(ps(b), src[:, ft % 4, tt * 128:(tt + 1) * 128], Wout_bf[:, ft, nh * 512:(nh + 1) * 512],
                   ["attnT", "ssmT", "Wout"], [("ps", b)], start=ft == 0, stop=ft == 7)
            TT("dve", h_sb[:, tt, nh * 512:(nh + 1) * 512], ps(b), xb_[:, nh * 512:(nh + 1) * 512], ALU.add,
               [("ps", b), ("xs2", tt % 2)], [("h", tt)])
        ACT(hb16[:], h_sb[:, tt, :], AF.Square, [("h", tt)], ["hb16", "ss2"], accum_out=ss2[:, 0:1])
        TSc("dve", ss2[:, 1:2], ss2[:, 0:1], 1.0 / D, ALU.mult, ["ss2"], ["ss2b"], s2=EPS, o1=ALU.add)
        TT("pool", ss2[:, 1:2], ss2[:, 1:2], neghalf[:, 0:1], ALU.pow, ["ss2b", "neghalf"], ["ss2b"])
        STT(hb16[:], h_sb[:, tt, :], ss2[:, 1:2], g2row[:], ALU.mult, ALU.mult, [("h", tt), "ss2b", "g2row", "hb16"],
            ["hb16"])
        for kt in range(8):
            if kt % 4 == 0:
                b = nbank((4, 5, 6, 7))
            op("pe", "transpose", ["hb16", "ident_bf"], [("ps", b)], out=psbf(b)[:, (kt % 4) * 128:(kt % 4 + 1) * 128],
               in_=hb16[:, kt * 128:(kt + 1) * 128], identity=ident_bf[:])
            if kt % 4 == 3:
                kt0 = kt - 3
                op("act", "copy", [("ps", b)], [("h2nT", tt)],
                   out=h2nT[:, kt0:kt0 + 4, tt * 128:(tt + 1) * 128],
                   in_=psbf(b)[:, 0:512].rearrange("p (k t) -> p k t", k=4))

    P.barrier()
    dump("h_sb", h_sb, []); dump("h2nT", h2nT, [])
    P.barrier()

    Ac = Alloc(OFF["TB"][0], AS_HI)
    NFH = NFF // 2
    actT = Ac("actT", [128, NFH, 2048], BF16)
    Wd_bf = Ac("Wd_bf", [128, NFH, D], BF16)
    Ad = Alloc(H_END, 207 * KB)
    Wg = [Ad("Wg%d" % i, [128, 8, 128], BF16) for i in range(2)]
    Wu = [Ad("Wu%d" % i, [128, 8, 128], BF16) for i in range(2)]
    sgl = [Ad("sgl%d" % i, [128, 512], F32) for i in range(2)]
    fin = [Ad("fin%d" % i, [128, D], F32) for i in range(2)]
    g2r2 = Ad("g2r2", [128, D], F32)
    ss3 = Ad("ss3", [128, 4], F32)
    jk3 = Ad("jk3", [128, D], BF16)
    P.dma("sp", g2r2[:], fgrow_d.partition_broadcast(128).rearrange("p o f -> p (o f)"), [], ["fgrow2"], "c1")
    for fh in range(2):
        for fl in range(NFH):
            P.dma("pool", Wd_bf[:, fl, :], w_down[(fh * NFH + fl) * 128:(fh * NFH + fl + 1) * 128, :],
                  [], [("Wd", fl)], "wd")
        for fl in range(NFH):
            f0 = (fh * NFH + fl) * 128
            wb = fl % 2
            P.dma("pool", Wg[wb][:], w_gate[:, f0:f0 + 128].rearrange("(k p) f -> p k f", p=128), [], [("Wg", wb)],
                  "wg%d" % wb)
            P.dma("pool", Wu[wb][:], w_up[:, f0:f0 + 128].rearrange("(k p) f -> p k f", p=128), [], [("Wu", wb)],
                  "wu%d" % wb)
            for tb in range(SO // 512):
                bg, bu = nbank((0, 1, 2, 3)), nbank((0, 1, 2, 3))
                hk = [("h2nT", tt) for tt in range(4 * tb, 4 * tb + 4)]
                for kt in range(8):
                    MM(ps(bg), Wg[wb][:, kt, :], h2nT[:, kt, tb * 512:(tb + 1) * 512], hk + [("Wg", wb)], [("ps", bg)],
                       start=kt == 0, stop=kt == 7)
                for kt in range(8):
                    MM(ps(bu), Wu[wb][:, kt, :], h2nT[:, kt, tb * 512:(tb + 1) * 512], hk + [("Wu", wb)], [("ps", bu)],
                       start=kt == 0, stop=kt == 7)
                s_ = sgl[tb % 2]
                ACT(s_[:], ps(bg), AF.Silu, [("ps", bg)], [("sgl", tb % 2)])
                TT("dve", actT[:, fl, tb * 512:(tb + 1) * 512], s_[:], ps(bu), ALU.mult, [("sgl", tb % 2), ("ps", bu)],
                   [("actT", fl)])
        for tt in range(NTO):
            for nh in range(2):
                b = nbank((4, 5, 6, 7))
                for fl in range(NFH):
                    MM(ps(b), actT[:, fl, tt * 128:(tt + 1) * 128], Wd_bf[:, fl, nh * 512:(nh + 1) * 512],
                       [("actT", fl), ("Wd", fl)], [("ps", b)], start=fl == 0, stop=fl == NFH - 1)
                TT("dve", h_sb[:, tt, nh * 512:(nh + 1) * 512], ps(b), h_sb[:, tt, nh * 512:(nh + 1) * 512], ALU.add,
                   [("ps", b), ("h", tt)], [("h", tt)])
    for tt in range(NTO):
        f_ = fin[tt % 2]
        ACT(jk3[:], h_sb[:, tt, :], AF.Square, [("h", tt)], ["jk3", "ss3"], accum_out=ss3[:, 0:1])
        TSc("dve", ss3[:, 1:2], ss3[:, 0:1], 1.0 / D, ALU.mult, ["ss3"], ["ss3b"], s2=EPS, o1=ALU.add)
        TT("pool", ss3[:, 1:2], ss3[:, 1:2], neghalf[:, 0:1], ALU.pow, ["ss3b", "neghalf"], ["ss3b"])
        STT(f_[:], h_sb[:, tt, :], ss3[:, 1:2], g2r2[:], ALU.mult, ALU.mult, [("h", tt), "ss3b", "fgrow2"],
            [("fin", tt % 2)])
        P.dma("sp", out_d[tt * 128:(tt + 1) * 128, :], f_[:], [("fin", tt % 2)], [("out", tt)], "outw")

    P.finish(final_sems=["outw"] + (["dbgw"] if dbg else []))
    st.close()
    return nc


_CACHE = {}


def _host_inputs(NJ, x, positions, norm1_g, w_in, lambda_q1, lambda_k1, lambda_q2, lambda_k2, subln_g,
                 ssm_lambda_re, ssm_lambda_im, ssm_log_step, ssm_b_re, ssm_b_im, ssm_c_re, ssm_c_im, ssm_d,
                 ssm_w_glu, ssm_b_glu, w_out, norm2_g, w_gate, w_up, w_down, final_g):
    f = lambda a: np.ascontiguousarray(np.asarray(a, dtype=np.float32))
    S = 4096 * NJ
    NT = S // 128
    x2 = f(x).reshape(S, D)
    pos = np.asarray(positions).reshape(S).astype(np.int32)
    invf = (500000.0 ** (-np.arange(0, 16, 2, dtype=np.float32) / 16)).astype(np.float32)
    lre, lim, lst = f(ssm_lambda_re)[0], f(ssm_lambda_im)[0], f(ssm_log_step)[0]

    def pair(a):
        return np.ascontiguousarray(a.reshape(16, 2, 64).transpose(1, 2, 0).reshape(128, 16))

    cmask = np.zeros((128, 130), np.float32)
    pidx = np.arange(128)
    cols = np.arange(128)
    cmask[:, :128] = ((pidx[:, None] // 64) == ((cols[None, :] // 16) % 2)).astype(np.float32)
    cmask[:, 128] = ((pidx // 16) % 2 == 0)
    cmask[:, 129] = ((pidx // 16) % 2 == 1)
    common = {
        "xa": x2,
        "posa": np.ascontiguousarray(pos.reshape(NT, 128).T),
        "invf": np.ascontiguousarray(np.broadcast_to(invf[None, :], (128, 8))),
        "w_in": f(w_in)[0],
        "g1": np.ascontiguousarray(f(norm1_g)[0].reshape(8, 128).T),
        "lamv": np.concatenate([f(lambda_q1)[0], f(lambda_k1)[0], f(lambda_q2)[0], f(lambda_k2)[0]])[None, :],
        "subg": f(subln_g)[0].reshape(128, 1),
        "lr_nat": np.ascontiguousarray(lre.T), "li_nat": np.ascontiguousarray(lim.T),
        "ls_nat": np.ascontiguousarray(np.broadcast_to(lst[None, :], (64, 32))),
        "lr_pair": pair(lre), "li_pair": pair(lim),
        "ls_pair": pair(np.ascontiguousarray(np.broadcast_to(lst[:, None], (32, 64)))),
        "b_re_nat": np.ascontiguousarray(f(ssm_b_re)[0].transpose(1, 0, 2).reshape(64, 512)),
        "b_im_nat": np.ascontiguousarray(f(ssm_b_im)[0].transpose(1, 0, 2).reshape(64, 512)),
        "c_re_rows": np.ascontiguousarray(f(ssm_c_re)[0].reshape(4, 128, 64).transpose(1, 0, 2)),
        "c_im_rows": np.ascontiguousarray(f(ssm_c_im)[0].reshape(4, 128, 64).transpose(1, 0, 2)),
        "dcol": np.ascontiguousarray(f(ssm_d)[0].reshape(4, 128).T),
        "w_glu": f(ssm_w_glu)[0],
        "bglu": np.ascontiguousarray(f(ssm_b_glu)[0].reshape(4, 128).T),
        "w_out": f(w_out)[0],
        "g2row": f(norm2_g)[0][None, :],
        "w_gate": f(w_gate)[0], "w_up": f(w_up)[0], "w_down": f(w_down)[0],
        "fgrow": f(final_g)[None, :],
        "cmask": cmask,
    }
    maps = []
    for c in range(NCORES):
        rows = np.concatenate([np.arange((8 * j + c) * 512, (8 * j + c + 1) * 512) for j in range(NJ)])
        cm = np.zeros((128, 64), np.float32)
        for r in range(32):
            cm[:, r] = 0.0 if r < 4 * c else NEG
        cm[:, 32 + c] = 1.0
        m = dict(common)
        m["xo"] = np.ascontiguousarray(x2[rows])
        m["poso"] = np.ascontiguousarray(pos[rows].reshape(4 * NJ, 128).T)
        m["cmeta"] = cm
        maps.append(m)
    return maps


DBG = False
LAST = {}


def kernel(**inputs):
    x = np.asarray(inputs["x"])
    S = x.shape[1]
    NJ = S // 4096
    if NJ not in _CACHE:
        _CACHE[NJ] = build(NJ, dbg=DBG)
    nc = _CACHE[NJ]
    maps = _host_inputs(NJ, **inputs)
    res = run_bass_kernel_spmd(nc, maps, core_ids=list(range(NCORES)))
    if DBG:
        LAST["res"] = res.results
    out = np.zeros((S, D), np.float32)
    for c in range(NCORES):
        o = np.asarray(res.results[c]["out"])
        for j in range(NJ):
            out[(8 * j + c) * 512:(8 * j + c + 1) * 512] = o[j * 512:(j + 1) * 512]
    return out.reshape(1, S, D)
```

```python
import contextlib
import math

import numpy as np
import ml_dtypes

import concourse.bass as bass
import concourse.mybir as mybir
from concourse.bass_utils import run_bass_kernel_spmd

F32, BF16, I32 = mybir.dt.float32, mybir.dt.bfloat16, mybir.dt.int32
AF = mybir.ActivationFunctionType
ALU = mybir.AluOpType
AX = mybir.AxisListType

NCORES = 8
D = 1024
DFF = 2816
NFF = DFF // 128
EPS = 1e-5
NEG = -30000.0
LAMBDA_INIT = 0.8 - 0.6 * math.exp(-0.3 * 0)
TWO_PI = 2.0 * math.pi
C1 = 6.28125
C2 = TWO_PI - C1
PI_SAFE = 3.141592
GELU_K1 = 2.0 * math.sqrt(2.0 / math.pi)
GELU_K2 = GELU_K1 * 0.044715

ENGS = ("pe", "act", "dve", "pool", "sp")


class _Op:
    __slots__ = ("eng", "fn", "reads", "writes", "dma_sem", "idx", "eidx", "deps",
                 "flag", "semval", "dma_cnt", "dmawait", "is_bar")

    def __init__(self, eng, fn, reads, writes, dma_sem):
        self.eng, self.fn, self.reads, self.writes, self.dma_sem = eng, fn, reads, writes, dma_sem
        self.deps = []
        self.flag = False
        self.semval = 0
        self.dma_cnt = 0
        self.dmawait = {}
        self.is_bar = False


class Prog:
    def __init__(self, nc):
        self.nc = nc
        self.ops = []
        self.per_eng = {e: [] for e in ENGS}
        self.last_w = {}
        self.readers = {}
        self.dma_counts = {}

    def op(self, eng, fn, reads=(), writes=(), dma_sem=None):
        o = _Op(eng, fn, tuple(reads), tuple(writes), dma_sem)
        o.idx = len(self.ops)
        o.eidx = len(self.per_eng[eng])
        deps = set()
        for k in o.reads:
            w = self.last_w.get(k)
            if w is not None:
                deps.add(w)
        for k in o.writes:
            w = self.last_w.get(k)
            if w is not None:
                deps.add(w)
            for r in self.readers.get(k, ()):
                deps.add(r)
        deps.discard(o)
        o.deps = sorted(deps, key=lambda d: d.idx)
        o.dmawait = {d.dma_sem: 16 * self.dma_counts[d.dma_sem] for d in o.deps if d.dma_sem is not None}
        for k in o.writes:
            self.last_w[k] = o
            self.readers[k] = []
        for k in o.reads:
            self.readers.setdefault(k, []).append(o)
        if dma_sem is not None:
            c = self.dma_counts.get(dma_sem, 0) + 1
            self.dma_counts[dma_sem] = c
            o.dma_cnt = c
        self.ops.append(o)
        self.per_eng[eng].append(o)
        return o

    def dma(self, eng, out, in_, reads, writes, sem):
        if sem == "c0":
            self.nc0 = getattr(self, "nc0", 0) + 1
            sem = "c0_%d" % self.nc0
        return self.op(eng, lambda e: e.dma_start(out=out, in_=in_), reads, writes, dma_sem=sem)

    def barrier(self):
        lasts = []
        for e in ENGS:
            for o in reversed(self.per_eng[e]):
                if o.dma_sem is None and o.fn is not None:
                    lasts.append(o)
                    break
        lastdma = {}
        for o in self.ops:
            if o.dma_sem is not None:
                lastdma[o.dma_sem] = o
        for e in ENGS:
            o = _Op(e, None, (), (), None)
            o.idx = len(self.ops)
            o.eidx = len(self.per_eng[e])
            o.is_bar = True
            o.deps = list(lasts) + list(lastdma.values())
            o.dmawait = {s: 16 * self.dma_counts[s] for s in lastdma}
            self.ops.append(o)
            self.per_eng[e].append(o)

    def finish(self, final_sems=()):
        nc = self.nc
        for o in self.ops:
            keep = []
            for d in o.deps:
                if d.dma_sem is None and d.eng == o.eng and not o.is_bar:
                    if o.eng == "pe" or d.eng == "sp":
                        continue
                    raw = any(k in d.writes for k in o.reads) or any(k in d.writes for k in o.writes)
                    if not raw or (o.eidx - d.eidx) > 3:
                        continue
                keep.append(d)
            o.deps = keep
            for d in keep:
                if d.dma_sem is None:
                    d.flag = True
        cnt = {e: 0 for e in ENGS}
        for o in self.ops:
            if o.dma_sem is None and o.flag:
                cnt[o.eng] += 1
                o.semval = cnt[o.eng]
        dma_names = sorted(self.dma_counts.keys())
        with contextlib.ExitStack() as st:
            esem = {e: st.enter_context(nc.semaphore("sem_" + e)) for e in ENGS}
            dsem = {n: st.enter_context(nc.semaphore("dsem_" + str(n))) for n in dma_names}
            block = st.enter_context(nc.Block())
            handles = {"pe": "tensor", "act": "scalar", "dve": "vector", "pool": "gpsimd", "sp": "sync"}

            def emit(engname):
                def body(e):
                    known = {}
                    for o in self.per_eng[engname]:
                        want = {}
                        for d in o.deps:
                            if d.dma_sem is not None:
                                key = ("d", d.dma_sem)
                                val = o.dmawait[d.dma_sem]
                            else:
                                key = ("e", d.eng)
                                val = d.semval
                            if val > want.get(key, 0):
                                want[key] = val
                        for key, val in want.items():
                            if known.get(key, 0) >= val:
                                continue
                            known[key] = val
                            e.wait_ge(dsem[key[1]] if key[0] == "d" else esem[key[1]], val)
                        if o.fn is None:
                            continue
                        ins = o.fn(e)
                        if o.dma_sem is not None:
                            ins.then_inc(dsem[o.dma_sem], 16)
                        elif o.flag:
                            ins.then_inc(esem[engname], 1)
                    if engname == "sp":
                        for name in final_sems:
                            e.wait_ge(dsem[name], 16 * self.dma_counts[name])
                return body

            for engname in ENGS:
                if self.per_eng[engname] or engname == "sp":
                    getattr(block, handles[engname])(emit(engname))


def build(NJ, dbg=False):
    S = 4096 * NJ
    NB = 8 * NJ
    NT = S // 128
    SO = 512 * NJ
    NTO = SO // 128
    nc = bass.Bass("TRN2", target_bir_lowering=False)
    P = Prog(nc)

    def din(name, shape, dt=F32):
        return nc.dram_tensor(name, list(shape), dt, kind="ExternalInput").ap()

    xa = din("xa", [S, D])
    xo = din("xo", [SO, D])
    posa = din("posa", [128, NT], I32)
    poso = din("poso", [128, NTO], I32)
    invf_d = din("invf", [128, 8])
    w_in = din("w_in", [D, 2048])
    g1_d = din("g1", [128, 8])
    lamv_d = din("lamv", [1, 256])
    subg_d = din("subg", [128, 1])
    lr_nat_d = din("lr_nat", [64, 32]); li_nat_d = din("li_nat", [64, 32]); ls_nat_d = din("ls_nat", [64, 32])
    lr_pair_d = din("lr_pair", [128, 16]); li_pair_d = din("li_pair", [128, 16]); ls_pair_d = din("ls_pair", [128, 16])
    b_re_d = din("b_re_nat", [64, 512]); b_im_d = din("b_im_nat", [64, 512])
    c_re_d = din("c_re_rows", [128, 4, 64]); c_im_d = din("c_im_rows", [128, 4, 64])
    dcol_d = din("dcol", [128, 4])
    w_glu = din("w_glu", [512, 512])
    bglu_d = din("bglu", [128, 4])
    w_out = din("w_out", [D, D])
    g2row_d = din("g2row", [1, D])
    w_gate = din("w_gate", [D, DFF]); w_up = din("w_up", [D, DFF]); w_down = din("w_down", [DFF, D])
    fgrow_d = din("fgrow", [1, D])
    cmask_d = din("cmask", [128, 130])
    cmeta_d = din("cmeta", [128, 64])
    out_d = nc.dram_tensor("out", [SO, D], F32, kind="ExternalOutput").ap()
    dbg_outs = {}

    dkw = dict(kind="ExternalOutput") if dbg else {}
    KT_all = nc.dram_tensor("KT_all", [128, 4, S], BF16, **dkw).ap()
    V_all = nc.dram_tensor("V_all", [4, 128, NT, 128], BF16, **dkw).ap()
    KT_ownd = nc.dram_tensor("KT_ownd", [128, 4, SO], BF16, **dkw).ap()
    V_ownd = nc.dram_tensor("V_ownd", [4, 128, NTO, 128], BF16, **dkw).ap()
    dbg_list = []

    def dump(name, t, keys):
        if not dbg:
            return
        shp = list(t.shape)
        dd = nc.dram_tensor("dbg_" + name, shp, t.dtype, kind="ExternalOutput").ap()
        P.dma("sp", dd, t[:], keys, [("dbg", name)], "dbgw")
        dbg_list.append(name)

    st = contextlib.ExitStack()
    arena = st.enter_context(nc.sbuf_tensor("arena", [128, 207 * 1024], mybir.dt.uint8))
    ABASE = nc.sbuf_base - 207 * 1024
    psum = st.enter_context(nc.psum_tensor("psum", [128, 8, 512], F32))
    base = 0
    KB = 1024

    OFF = {}
    uniq = [0]

    class Alloc:
        def __init__(self, start, end):
            self.cur, self.end = start, end

        def __call__(self, name, shape, dt):
            nbytes = int(np.prod(shape[1:])) * (4 if dt in (F32, I32) else 2)
            nbytes = (nbytes + 31) // 32 * 32
            uniq[0] += 1
            t = nc.alloc_sbuf_tensor_at("%s_%d" % (name, uniq[0]), list(shape), dt, offset=ABASE + self.cur)
            OFF[name] = (self.cur, nbytes)
            self.cur += nbytes
            assert self.cur <= self.end, (name, self.cur, self.end)
            return t

    A0 = Alloc(0, 8 * KB)
    ident_bf = A0("ident_bf", [128, 128], BF16)
    ident_f = A0("ident_f", [128, 128], F32)
    ones_bf = A0("ones_bf", [128, 128], BF16)
    ones_f = A0("ones_f", [128, 128], F32)
    cmask = A0("cmask", [128, 130], F32)
    cmeta = A0("cmeta", [128, 64], F32)
    nMbd = A0("nMbd", [128, 128], F32)
    neghalf = A0("neghalf", [128, 512], F32)
    small = A0("small", [128, 64], F32)
    g1 = A0("g1", [128, 8], F32)
    gcol = A0("gcol", [128, 1], F32)
    dcol = A0("dcolt", [128, 4], F32)
    bglu = A0("bglut", [128, 4], F32)
    invf = A0("invft", [128, 8], F32)

    A1 = Alloc(8 * KB, 116 * KB)
    QT_own = A1("QT_own", [128, 4, 2048], BF16)
    uT_own = A1("uT_own", [128, 4, 2048], BF16)
    TB = A1("TB", [128, 4, 8, 2, 128], BF16)
    CT = A1("CT", [128, 4, 2, 128], BF16)
    W8 = A1("W8", [128, 16, 2, 64], F32)
    PW = A1("PW", [128, 10, 3, 16], F32)
    Et = A1("Et", [128, 2, 32, 16], F32)
    Sp = A1("Sp", [128, 2, 33, 16], F32)
    Sown = A1("Sown", [128, 2, 4, 16], F32)
    Wglu_bf = A1("Wglu_bf", [128, 4, 512], BF16)
    attnT = A1("attnT", [128, 4, 2048], BF16)
    ssmT = A1("ssmT", [128, 4, 2048], BF16)
    T0 = A1.cur
    AS_LO = OFF["attnT"][0]
    AS_HI = OFF["ssmT"][0] + OFF["ssmT"][1]

    ps = lambda b: psum[:, b, :]
    psbf = lambda b: psum[:, b, :].bitcast(BF16)

    def op(eng, method, reads, writes, **kw):
        return P.op(eng, lambda e: getattr(e, method)(**kw), reads, writes)

    def TT(eng, out, in0, in1, o, r, w):
        return op(eng, "tensor_tensor", r, w, out=out, in0=in0, in1=in1, op=o)

    def TSc(eng, out, in0, s1, o0, r, w, s2=None, o1=None):
        kw = dict(out=out, in0=in0, scalar1=s1, scalar2=s2, op0=o0)
        if o1 is not None:
            kw["op1"] = o1
        return op(eng, "tensor_scalar", r, w, **kw)

    def STT(out, in0, scalar, in1, o0, o1, r, w):
        return op("dve", "scalar_tensor_tensor", r, w, out=out, in0=in0, scalar=scalar, in1=in1, op0=o0, op1=o1)

    def ACT(out, in_, func, r, w, **kw):
        return op("act", "activation", r, w, out=out, in_=in_, func=func, **kw)

    def MM(out, lhsT, rhs, r, w, start=True, stop=True, **kw):
        return op("pe", "matmul", r, w, out=out, lhsT=lhsT, rhs=rhs, start=start, stop=stop, **kw)

    P.dma("sp", cmask[:], cmask_d[:, :], [], ["cmask"], "c0")
    P.dma("sp", cmeta[:], cmeta_d[:, :], [], ["cmeta"], "c0")
    P.dma("sp", g1[:], g1_d[:, :], [], ["g1"], "c0")
    P.dma("sp", dcol[:], dcol_d[:, :], [], ["dcol"], "c0")
    P.dma("sp", bglu[:], bglu_d[:, :], [], ["bglu"], "c0")
    P.dma("sp", invf[:], invf_d[:, :], [], ["invf"], "c0")
    P.dma("sp", gcol[:], subg_d[:, :], [], ["gcol"], "c0")
    op("pool", "memset", [], ["ones_bf"], ap=ones_bf[:], constant=1.0)
    op("pool", "memset", [], ["ones_f"], ap=ones_f[:], constant=1.0)
    op("pool", "memset", [], ["neghalf"], ap=neghalf[:], constant=-0.5)
    At = Alloc(T0, 207 * KB)
    Win = At("Win", [128, 8, 2048], BF16)
    ropecs = {}
    for key_, nt_ in (("ra", NT), ("ro", NTO)):
        ropecs[key_] = (At(key_ + "_sn", [128, nt_ * 8], F32), At(key_ + "_cs", [128, nt_ * 8], F32))
    T1 = At.cur
    A = Alloc(T1, 207 * KB)
    AS = Alloc(AS_LO, AS_HI)
    io_t = A("io_t", [128, 128], I32)
    io_f = A("io_f", [128, 128], F32)
    op("pool", "iota", [], ["io_t"], out=io_t[:], pattern=[[1, 128]], base=0, channel_multiplier=-1)
    op("dve", "tensor_copy", ["io_t"], ["io_f"], out=io_f[:], in_=io_t[:])
    TSc("dve", ident_f[:], io_f[:], 0.0, ALU.is_equal, ["io_f"], ["ident_f"])
    op("dve", "tensor_copy", ["ident_f"], ["ident_bf"], out=ident_bf[:], in_=ident_f[:])
    TSc("dve", nMbd[:], cmask[:, 0:128], -1.0, ALU.mult, ["cmask"], ["nMbd"])
    TSc("dve", gcol[:], gcol[:], 1.0 - LAMBDA_INIT, ALU.mult, ["gcol"], ["gcol"])

    lamb = A("lamb", [128, 256], F32)
    lamp = A("lamp", [128, 128], F32)
    P.dma("sp", lamb[:], lamv_d.partition_broadcast(128).rearrange("p o f -> p (o f)"), [], ["lamb"], "c0")
    TT("dve", lamp[:, 0:64], lamb[:, 0:64], lamb[:, 64:128], ALU.mult, ["lamb"], ["lamp"])
    TT("dve", lamp[:, 64:128], lamb[:, 128:192], lamb[:, 192:256], ALU.mult, ["lamb"], ["lamp"])
    op("dve", "tensor_reduce", ["lamp"], ["small"], out=small[:, 2:4],
       in_=lamp[:].rearrange("p (a b) -> p a b", a=2), axis=AX.X, op=ALU.add)
    ACT(small[:, 4:6], small[:, 2:4], AF.Exp, ["small"], ["small"])
    TT("dve", small[:, 6:7], small[:, 5:6], small[:, 4:5], ALU.subtract, ["small"], ["small"])
    TSc("dve", small[:, 0:1], small[:, 6:7], -LAMBDA_INIT, ALU.add, ["small"], ["small"])
    neglam = small[:, 0:1]

    def sincos(theta, shape, key, tmpA, outs=None):
        n = int(np.prod(shape[1:]))
        pcount = shape[0]
        yi = tmpA(key + "_yi", [128, n], I32)
        yf = tmpA(key + "_yf", [128, n], F32)
        r1 = tmpA(key + "_r1", [128, n], F32)
        r2 = tmpA(key + "_r2", [128, n], F32)
        if outs is None:
            sn = tmpA(key + "_sn", [128, n], F32)
            cs = tmpA(key + "_cs", [128, n], F32)
        else:
            sn, cs = outs
        th = theta
        k = key
        sl = slice(0, pcount)
        TSc("dve", yf[sl], th, 1.0 / TWO_PI, ALU.mult, [k + "th"], [k + "yf"])
        op("dve", "tensor_copy", [k + "yf"], [k + "yi"], out=yi[sl], in_=yf[sl])
        op("dve", "tensor_copy", [k + "yi"], [k + "yf"], out=yf[sl], in_=yi[sl])
        STT(r1[sl], yf[sl], -C1, th, ALU.mult, ALU.add, [k + "yf", k + "th"], [k + "r1"])
        STT(r1[sl], yf[sl], -C2, r1[sl], ALU.mult, ALU.add, [k + "yf", k + "r1"], [k + "r1"])
        TSc("dve", r1[sl], r1[sl], PI_SAFE, ALU.min, [k + "r1"], [k + "r1"], s2=-PI_SAFE, o1=ALU.max)
        ACT(sn[sl], r1[sl], AF.Sin, [k + "r1"], [k + "sn"])
        TSc("dve", r2[sl], r1[sl], math.pi / 2, ALU.add, [k + "r1"], [k + "r2"])
        TSc("dve", yf[sl], r2[sl], math.pi, ALU.is_gt, [k + "r2"], [k + "yf"])
        STT(r2[sl], yf[sl], -TWO_PI, r2[sl], ALU.mult, ALU.add, [k + "yf", k + "r2"], [k + "r2"])
        TSc("dve", r2[sl], r2[sl], PI_SAFE, ALU.min, [k + "r2"], [k + "r2"], s2=-PI_SAFE, o1=ALU.max)
        ACT(cs[sl], r2[sl], AF.Sin, [k + "r2"], [k + "cs"])
        return sn, cs

    def rope_tables(pos_d, ntile, key):
        Ar = Alloc(T1 + 28 * KB, 207 * KB)
        posi = Ar("rp_pi", [128, ntile], I32)
        posf = Ar("rp_pf", [128, ntile], F32)
        th = Ar("rp_th", [128, ntile * 8], F32)
        P.dma("sp", posi[:], pos_d[:, :], [], [key + "pi"], "c0")
        op("dve", "tensor_copy", [key + "pi"], [key + "pf"], out=posf[:], in_=posi[:])
        TT("dve", th[:].rearrange("p (t f) -> p t f", f=8),
           posf[:].unsqueeze(2).broadcast_to([128, ntile, 8]),
           invf[:].unsqueeze(1).broadcast_to([128, ntile, 8]), ALU.mult, [key + "pf", "invf"], [key + "th"])
        sn, cs = sincos(th[:], [128, ntile * 8], key, Ar, outs=ropecs[key])
        return (cs[:].rearrange("p (t f) -> p t f", f=8), sn[:].rearrange("p (t f) -> p t f", f=8), key)

    ropeA = rope_tables(posa, NT, "ra")
    P.barrier()
    ropeO = rope_tables(poso, NTO, "ro")
    P.barrier()

    def a_chain(lr_d, li_d, ls_d, np_, nf, key):
        sl = slice(0, np_)
        t = {}
        for nm in ("lr", "li", "ls", "step", "e1", "mag", "th", "are", "aim", "den", "nr", "t1", "t2", "cre", "cim"):
            t[nm] = A(key + nm, [128, nf], F32)
        P.dma("sp", t["lr"][sl], lr_d[:, :], [], [key + "lr"], "c0")
        P.dma("sp", t["li"][sl], li_d[:, :], [], [key + "li"], "c0")
        P.dma("sp", t["ls"][sl], ls_d[:, :], [], [key + "ls"], "c0")
        k = key
        ACT(t["step"][sl], t["ls"][sl], AF.Exp, [k + "ls"], [k + "step"])
        TSc("dve", t["lr"][sl], t["lr"][sl], -1e-4, ALU.min, [k + "lr"], [k + "lr"])
        TT("dve", t["e1"][sl], t["lr"][sl], t["step"][sl], ALU.mult, [k + "lr", k + "step"], [k + "e1"])
        ACT(t["mag"][sl], t["e1"][sl], AF.Exp, [k + "e1"], [k + "mag"])
        TT("dve", t["th"][sl], t["li"][sl], t["step"][sl], ALU.mult, [k + "li", k + "step"], [k + "th" + "th"])
        sn, cs = sincos(t["th"][sl], [np_, nf], k + "th", A)
        TT("dve", t["are"][sl], t["mag"][sl], cs[sl], ALU.mult, [k + "mag", k + "thcs"], [k + "are"])
        TT("dve", t["aim"][sl], t["mag"][sl], sn[sl], ALU.mult, [k + "mag", k + "thsn"], [k + "aim"])
        TT("dve", t["t1"][sl], t["lr"][sl], t["lr"][sl], ALU.mult, [k + "lr"], [k + "t1"])
        TT("dve", t["t2"][sl], t["li"][sl], t["li"][sl], ALU.mult, [k + "li"], [k + "t2"])
        TT("dve", t["den"][sl], t["t1"][sl], t["t2"][sl], ALU.add, [k + "t1", k + "t2"], [k + "den"])
        op("dve", "reciprocal", [k + "den"], [k + "den"], out=t["den"][sl], in_=t["den"][sl])
        TSc("dve", t["nr"][sl], t["are"][sl], -1.0, ALU.add, [k + "are"], [k + "nr"])
        TT("dve", t["t1"][sl], t["nr"][sl], t["lr"][sl], ALU.mult, [k + "nr", k + "lr"], [k + "t1"])
        TT("dve", t["t2"][sl], t["aim"][sl], t["li"][sl], ALU.mult, [k + "aim", k + "li"], [k + "t2"])
        TT("dve", t["cre"][sl], t["t1"][sl], t["t2"][sl], ALU.add, [k + "t1", k + "t2"], [k + "cre"])
        TT("dve", t["cre"][sl], t["cre"][sl], t["den"][sl], ALU.mult, [k + "cre", k + "den"], [k + "cre"])
        TT("dve", t["t1"][sl], t["aim"][sl], t["lr"][sl], ALU.mult, [k + "aim", k + "lr"], [k + "t1"])
        TT("dve", t["t2"][sl], t["nr"][sl], t["li"][sl], ALU.mult, [k + "nr", k + "li"], [k + "t2"])
        TT("dve", t["cim"][sl], t["t1"][sl], t["t2"][sl], ALU.subtract, [k + "t1", k + "t2"], [k + "cim"])
        TT("dve", t["cim"][sl], t["cim"][sl], t["den"][sl], ALU.mult, [k + "cim", k + "den"], [k + "cim"])
        return t

    tmpc1 = AS("tmpc1", [128, 2048], F32)
    tmpc2 = AS("tmpc2", [128, 2048], F32)

    def cmul(o_re, o_im, a_re, a_im, b_re, b_im, shape, rk, wk):
        n = int(np.prod(shape[1:]))
        sl = slice(0, shape[0])
        x1 = tmpc1[sl, 0:n]
        x2 = tmpc2[sl, 0:n]
        if len(shape) == 3:
            x1 = x1.rearrange("p (a b) -> p a b", a=shape[1])
            x2 = x2.rearrange("p (a b) -> p a b", a=shape[1])
        TT("dve", x1, a_re, b_re, ALU.mult, rk, ["tmpc1"])
        TT("dve", x2, a_im, b_im, ALU.mult, rk, ["tmpc2"])
        TT("dve", o_re, x1, x2, ALU.subtract, ["tmpc1", "tmpc2"], wk)
        TT("dve", x1, a_re, b_im, ALU.mult, rk, ["tmpc1"])
        TT("dve", x2, a_im, b_re, ALU.mult, rk, ["tmpc2"])
        TT("dve", o_im, x1, x2, ALU.add, ["tmpc1", "tmpc2"], wk)

    cp = a_chain(lr_pair_d, li_pair_d, ls_pair_d, 128, 16, "cp")
    op("dve", "tensor_copy", ["cpare"], ["PW"], out=PW[:, 0, 0, :], in_=cp["are"][:])
    op("dve", "tensor_copy", ["cpaim"], ["PW"], out=PW[:, 0, 1, :], in_=cp["aim"][:])
    for k in range(1, 10):
        cmul(PW[:, k, 0, :], PW[:, k, 1, :], PW[:, k - 1, 0, :], PW[:, k - 1, 1, :],
             PW[:, k - 1, 0, :], PW[:, k - 1, 1, :], [128, 16], ["PW"], ["PW"])
    TSc("dve", PW[:, :, 2, :], PW[:, :, 1, :], -1.0, ALU.mult, ["PW"], ["PW"])
    op("pool", "memset", [], ["W8"], ap=W8[:, :, 0, :], constant=1.0)
    op("pool", "memset", ["W8"], ["W8"], ap=W8[:, :, 1, :], constant=0.0)
    op("dve", "tensor_copy", ["PW", "W8"], ["W8"], out=W8[:, :, 0, 62:63], in_=PW[:, 3, 0, :].unsqueeze(2))
    op("dve", "tensor_copy", ["PW", "W8"], ["W8"], out=W8[:, :, 1, 62:63], in_=PW[:, 3, 1, :].unsqueeze(2))
    lo = 62
    for k in range(4, 9):
        n = 64 - lo
        nlo = lo - n
        bre = PW[:, k, 0, :].unsqueeze(2).broadcast_to([128, 16, n])
        bim = PW[:, k, 1, :].unsqueeze(2).broadcast_to([128, 16, n])
        cmul(W8[:, :, 0, nlo:lo], W8[:, :, 1, nlo:lo], W8[:, :, 0, lo:64], W8[:, :, 1, lo:64], bre, bim,
             [128, 16, n], ["W8", "PW"], ["W8"])
        lo = nlo
    assert lo == 0

    cn = a_chain(lr_nat_d, li_nat_d, ls_nat_d, 64, 32, "cn")
    Bn = [AS("Bn%d" % i, [128, 512], F32) for i in range(2)]
    Bb = [AS("Bb%d" % i, [128, 512], F32) for i in range(2)]
    Wv = [AS("Wv%d" % i, [128, 512], F32) for i in range(2)]
    apn = [[A("apn%d_%d" % (v, i), [128, 32], F32) for i in range(2)] for v in range(8)]
    P.dma("sp", Bn[0][0:64], b_re_d[:, :], [], ["Bn"], "c0")
    P.dma("sp", Bn[1][0:64], b_im_d[:, :], [], ["Bn"], "c0")
    v3 = lambda t: t[0:64, :].rearrange("p (g h) -> p g h", h=16)
    b3 = lambda t: t[0:64, :].unsqueeze(2).broadcast_to([64, 32, 16])
    cmul(v3(Bb[0]), v3(Bb[1]), b3(cn["cre"]), b3(cn["cim"]), v3(Bn[0]), v3(Bn[1]), [64, 32, 16],
         ["Bn", "cncre", "cncim"], ["Bb"])
    op("pool", "memset", [], ["apn"], ap=apn[0][0][0:64], constant=1.0)
    op("pool", "memset", [], ["apn"], ap=apn[0][1][0:64], constant=0.0)
    for v in range(1, 8):
        cmul(apn[v][0][0:64], apn[v][1][0:64], apn[v - 1][0][0:64], apn[v - 1][1][0:64],
             cn["are"][0:64], cn["aim"][0:64], [64, 32], ["apn", "cnare", "cnaim"], ["apn"])
    tb_bank = 0
    for v in range(8):
        if v == 0:
            src = Bb
        else:
            cmul(v3(Wv[0]), v3(Wv[1]), b3(apn[v][0]), b3(apn[v][1]), v3(Bb[0]), v3(Bb[1]), [64, 32, 16],
                 ["apn", "Bb"], ["Wv"])
            src = Wv
        for ri in range(2):
            for t in range(4):
                b = tb_bank % 8
                tb_bank += 1
                op("pe", "transpose", ["Wv", "Bb", "ident_f"], [("ps", b)], out=ps(b)[:, 0:64],
                   in_=src[ri][0:64, t * 128:(t + 1) * 128], identity=ident_f[0:64, 0:64])
                for e_ in range(2):
                    TSc("dve", TB[:, t, v, ri, 64 * e_:64 * e_ + 64], ps(b)[:, 0:64], cmask[:, 128 + e_:129 + e_],
                        ALU.mult, [("ps", b), "cmask"], ["TB"])
    Cd = [A("Cd%d" % i, [128, 4, 128], F32) for i in range(2)]
    for ri, cd in enumerate((c_re_d, c_im_d)):
        P.dma("sp", Cd[ri][:, :, 0:64], cd[:, :, :], [], ["Cd"], "c0")
        P.dma("sp", Cd[ri][:, :, 64:128], cd[:, :, :], [], ["Cd"], "c0")
        for t in range(4):
            b = tb_bank % 8
            tb_bank += 1
            op("pe", "transpose", ["Cd", "ident_f"], [("ps", b)], out=ps(b)[:, 0:128], in_=Cd[ri][:, t, :],
               identity=ident_f[:])
            TT("dve", CT[:, t, ri, :], ps(b)[:, 0:128], cmask[:, 0:128] if ri == 0 else nMbd[:], ALU.mult,
               [("ps", b), "cmask", "nMbd"], ["CT"])

    wst = A("wst", [128, 2048], F32)
    for kt in range(8):
        P.dma("sp", wst[:], w_in[kt * 128:(kt + 1) * 128, :], [], ["wst"], "wst")
        TSc("dve", Win[:, kt, :], wst[:], g1[:, kt:kt + 1], ALU.mult, ["wst", "g1"], [("Win", kt)])
    for ct in range(4):
        P.dma("pool", Wglu_bf[:, ct, :], w_glu[ct * 128:(ct + 1) * 128, :], [], ["Wglu"], "wglu")
    WinK = [("Win", kt) for kt in range(8)]
    assert A.cur <= T1 + 28 * KB, (A.cur, T1)

    P.barrier()
    A = Alloc(T1, 207 * KB)
    AS = Alloc(AS_LO, AS_HI)
    xst = A("xst", [128, 4, D], F32)
    xb = A("xb", [128, 4, D], BF16)
    hnT = A("hnT", [128, 8, 512], BF16)
    Kst = A("Kst", [128, 4, 512], BF16)
    Vst = A("Vst", [128, 4, 4, 128], BF16)
    Qst = A("Qst", [128, 4, 512], BF16)
    KTst = A("KTst", [128, 4, 512], BF16)
    rt = [A("rt%d" % i, [128, 64], F32) for i in range(4)]
    ssq = A("ssq", [128, 8], F32)
    uTh = AS("uTh", [128, 4, 2048], BF16)
    junk = AS("junk", [128, D], BF16)
    sred = [AS("sred%d" % i, [128, 256], F32) for i in range(4)]

    bankrr = [0]

    def nbank(allowed=(0, 1, 2, 3, 4, 5, 6, 7)):
        b = allowed[bankrr[0] % len(allowed)]
        bankrr[0] += 1
        return b

    def rope_evac(psb, dst, rope, ti, dk):
        cs, sn, rk = rope
        p3 = ps(psb).rearrange("p (m d) -> p m d", d=64)
        d3 = dst.rearrange("p (m d) -> p m d", d=64)
        cb = cs[:, ti, :].unsqueeze(1).broadcast_to([128, 8, 8])
        sb = sn[:, ti, :].unsqueeze(1).broadcast_to([128, 8, 8])
        rd = [("ps", psb), rk + "cs", rk + "sn"]
        r3 = [r[:].rearrange("p (m f) -> p m f", f=8) for r in rt]
        op("dve", "tensor_copy", [("ps", psb)], [dk], out=dst, in_=ps(psb))
        TT("dve", r3[0], p3[:, :, 0:8], cb, ALU.mult, rd, ["rt0"])
        TT("dve", r3[1], p3[:, :, 8:16], sb, ALU.mult, rd, ["rt1"])
        TT("dve", r3[2], p3[:, :, 8:16], cb, ALU.mult, rd, ["rt2"])
        TT("dve", r3[3], p3[:, :, 0:8], sb, ALU.mult, rd, ["rt3"])
        TT("dve", d3[:, :, 0:8], r3[0], r3[1], ALU.subtract, ["rt0", "rt1", dk], [dk])
        TT("dve", d3[:, :, 8:16], r3[2], r3[3], ALU.add, ["rt2", "rt3", dk], [dk])

    def stream_block(x_rows, rope, tile0, own, ublk_dst, ukey, kt_dst, v_dst, qdst=None):
        P.dma("sp", xst[:], x_rows.rearrange("(t p) d -> p t d", p=128), [], ["xst"], "xst")
        for t in range(4):
            ACT(junk[:], xst[:, t, :], AF.Square, ["xst"], ["junk", "ssq"], accum_out=ssq[:, t:t + 1])
        TSc("dve", ssq[:, 4:8], ssq[:, 0:4], 1.0 / D, ALU.mult, ["ssq"], ["ssq2"], s2=EPS, o1=ALU.add)
        TT("pool", ssq[:, 4:8], ssq[:, 4:8], neghalf[:, 0:4], ALU.pow, ["ssq2", "neghalf"], ["ssq2"])
        for t in range(4):
            TSc("pool", xb[:, t, :], xst[:, t, :], ssq[:, 4 + t:5 + t], ALU.mult, ["xst", "ssq2"], [("xb", t)])
        for kt in range(8):
            b = nbank((0, 1))
            for t in range(4):
                op("pe", "transpose", [("xb", t), "ident_bf"], [("ps", b)], out=psbf(b)[:, t * 128:(t + 1) * 128],
                   in_=xb[:, t, kt * 128:(kt + 1) * 128], identity=ident_bf[:])
            if kt % 2 == 0:
                op("act", "copy", [("ps", b)], [("hnT", kt)], out=hnT[:, kt, :], in_=psbf(b)[:, 0:512])
            else:
                op("dve", "tensor_copy", [("ps", b)], [("hnT", kt)], out=hnT[:, kt, :], in_=psbf(b)[:, 0:512])
        hk = [("hnT", kt) for kt in range(8)]
        for t in range(4):
            bk, bv = nbank((2, 3, 4, 5)), nbank((2, 3, 4, 5))
            for kt in range(8):
                lt = hnT[:, kt, t * 128:(t + 1) * 128]
                MM(ps(bk), lt, Win[:, kt, 512:1024], hk + WinK, [("ps", bk)], start=kt == 0, stop=kt == 7)
                MM(ps(bv), lt, Win[:, kt, 1024:1536], hk + WinK, [("ps", bv)], start=kt == 0, stop=kt == 7)
            if own:
                bq = nbank((2, 3, 4, 5))
                for kt in range(8):
                    MM(ps(bq), hnT[:, kt, t * 128:(t + 1) * 128], Win[:, kt, 0:512], hk + WinK, [("ps", bq)],
                       start=kt == 0, stop=kt == 7)
            op("act", "copy", [("ps", bv)], [("Vst", t)], out=Vst[:, :, t, :],
               in_=ps(bv).rearrange("p (h d) -> p h d", h=4))
            rope_evac(bk, Kst[:, t, :], rope, tile0 + t, ("Kst", t))
            if own:
                rope_evac(bq, Qst[:, t, :], rope, tile0 + t, ("Qst", t))
        for ct in range(4):
            b = nbank((6, 7))
            for kt in range(8):
                MM(ps(b), Win[:, kt, 1536 + ct * 128:1536 + (ct + 1) * 128], hnT[:, kt, :], hk + WinK, [("ps", b)],
                   start=kt == 0, stop=kt == 7)
            op("act", "copy", [("ps", b)], [ukey], out=ublk_dst[:, ct, :], in_=ps(b))
        for (srcst, skey, dstT, dkey) in ([(Kst, "Kst", None, None)] + ([(Qst, "Qst", qdst, "QT_own")] if own else [])):
            for ft in range(4):
                b = nbank((0, 1))
                for t in range(4):
                    op("pe", "transpose", [(skey, t), "ident_bf"], [("ps", b)],
                       out=psbf(b)[:, t * 128:(t + 1) * 128], in_=srcst[:, t, ft * 128:(ft + 1) * 128],
                       identity=ident_bf[:])
                if dstT is None:
                    op("dve", "tensor_copy", [("ps", b)], ["KTst"], out=KTst[:, ft, :], in_=psbf(b)[:, 0:512])
                else:
                    op("dve", "tensor_copy", [("ps", b)], [dkey], out=dstT[:, ft, :], in_=psbf(b)[:, 0:512])
        P.dma("sp", kt_dst[0], KTst[:], ["KTst"], [kt_dst[1]], "ktw")
        P.dma("sp", v_dst[0], Vst[:], [("Vst", t) for t in range(4)], [v_dst[1]], "vw")

    def ssm_prefix_half(hb):
        for q in range(16):
            p0 = 32 * (q % 4)
            ct = q // 4
            bb = [nbank((2, 3, 4, 5)), nbank((2, 3, 4, 5))]
            for ri in range(2):
                for s_ in range(8):
                    MM(ps(bb[ri])[:, 0:256], TB[p0:p0 + 32, ct, 7 - s_, ri, :], uTh[p0:p0 + 32, ct, s_:2048:8],
                       ["TB", "uTh"], [("ps", bb[ri])], start=s_ == 0, stop=s_ == 7, tile_position=(p0, 0))
            Lr = ps(bb[0])[:, 0:256].rearrange("p (b c) -> p b c", c=64)
            Li = ps(bb[1])[:, 0:256].rearrange("p (b c) -> p b c", c=64)
            Wr = W8[:, q, 0, :].unsqueeze(1).broadcast_to([128, 4, 64])
            Wi = W8[:, q, 1, :].unsqueeze(1).broadcast_to([128, 4, 64])
            s3 = [s[:].rearrange("p (b c) -> p b c", c=64) for s in sred]
            rd = [("ps", bb[0]), ("ps", bb[1]), "W8"]
            TT("dve", s3[0], Lr, Wr, ALU.mult, rd, ["sred0"])
            TT("dve", s3[1], Li, Wi, ALU.mult, rd, ["sred1"])
            TT("dve", s3[2], Lr, Wi, ALU.mult, rd, ["sred2"])
            TT("dve", s3[3], Li, Wr, ALU.mult, rd, ["sred3"])
            TT("dve", s3[0], s3[0], s3[1], ALU.subtract, ["sred0", "sred1"], ["sred0"])
            TT("dve", s3[2], s3[2], s3[3], ALU.add, ["sred2", "sred3"], ["sred2"])
            op("dve", "tensor_reduce", ["sred0"], ["Et"], out=Et[:, 0, 4 * hb:4 * hb + 4, q], in_=s3[0], axis=AX.X,
               op=ALU.add)
            op("dve", "tensor_reduce", ["sred2"], ["Et"], out=Et[:, 1, 4 * hb:4 * hb + 4, q], in_=s3[2], axis=AX.X,
               op=ALU.add)

    for j in range(NJ):
        stream_block(xo[j * 512:(j + 1) * 512, :], ropeO, 4 * j, True,
                     uT_own[:, :, j * 512:(j + 1) * 512], "uT_own",
                     (KT_ownd[:, :, j * 512:(j + 1) * 512], ("KTo", j)),
                     (V_ownd[:, :, 4 * j:4 * j + 4, :].rearrange("h p t d -> p h t d"), ("Vo", j)),
                     qdst=QT_own[:, :, j * 512:(j + 1) * 512])
    for i in range(NB):
        stream_block(xa[i * 512:(i + 1) * 512, :], ropeA, 4 * i, False,
                     uTh[:, :, (i % 4) * 512:(i % 4 + 1) * 512], "uTh",
                     (KT_all[:, :, i * 512:(i + 1) * 512], ("KTa", i)),
                     (V_all[:, :, 4 * i:4 * i + 4, :].rearrange("h p t d -> p h t d"), ("Va", i)))
        if i % 4 == 3:
            ssm_prefix_half(i // 4)

    op("pool", "memset", [], ["Sp"], ap=Sp[:, :, 0, :], constant=0.0)
    sr = [r[:, 0:16] for r in rt]
    for b in range(NB - 1):
        Are, Aim = PW[:, 9, 0, :], PW[:, 9, 1, :]
        TT("dve", sr[0], Sp[:, 0, b, :], Are, ALU.mult, ["Sp", "PW"], ["rt0"])
        TT("dve", sr[1], Sp[:, 1, b, :], Aim, ALU.mult, ["Sp", "PW"], ["rt1"])
        TT("dve", sr[2], Sp[:, 0, b, :], Aim, ALU.mult, ["Sp", "PW"], ["rt2"])
        TT("dve", sr[3], Sp[:, 1, b, :], Are, ALU.mult, ["Sp", "PW"], ["rt3"])
        TT("dve", sr[0], sr[0], sr[1], ALU.subtract, ["rt0", "rt1"], ["rt0"])
        TT("dve", sr[2], sr[2], sr[3], ALU.add, ["rt2", "rt3"], ["rt2"])
        TT("dve", Sp[:, 0, b + 1, :], sr[0], Et[:, 0, b, :], ALU.add, ["rt0", "Et"], ["Sp"])
        TT("dve", Sp[:, 1, b + 1, :], sr[2], Et[:, 1, b, :], ALU.add, ["rt2", "Et"], ["Sp"])
    op("pool", "memset", [], ["Sown"], ap=Sown[:], constant=0.0)
    for j in range(NJ):
        for ri in range(2):
            for i in range(8):
                STT(Sown[:, ri, j, :], Sp[:, ri, 8 * j + i, :], cmeta[:, 32 + i:33 + i], Sown[:, ri, j, :],
                    ALU.mult, ALU.add, ["Sp", "cmeta", "Sown"], ["Sown"])

    P.barrier()
    dump("QT_own", QT_own, []); dump("uT_own", uT_own, []); dump("Et", Et, []); dump("Sp", Sp, []); dump("Sown", Sown, [])
    dump("TB", TB, []); dump("CT", CT, []); dump("W8", W8, []); dump("PW", PW, []); dump("small", small, [])
    P.barrier()

    A = Alloc(T0, 207 * KB)
    NKB = 4
    KTc = [A("KTc%d" % i, [128, 1024], BF16) for i in range(NKB)]
    Vc = [A("Vc%d" % i, [128, 8, 128], BF16) for i in range(NKB)]
    PT = [A("PT%d" % i, [128, 2, 512], BF16) for i in range(3)]
    maskD = A("maskD", [128, 4, 512], BF16)
    mio = A("mio", [128, 512], I32)
    miof = A("miof", [128, 512], F32)
    lacc = A("lacc", [128, 2, 512], F32)
    Bsb = [A("Bsb%d" % i, [128, 512], F32) for i in range(2)]
    tm = [A("tm%d" % i, [128, 512], F32) for i in range(2)]
    Dt = A("Dt", [128, 512], F32)
    sq = A("sq", [128, 512], F32)
    msr = A("msr", [128, 512], F32)
    op("pool", "iota", [], ["mio"], out=mio[:], pattern=[[1, 512]], base=0, channel_multiplier=-1)
    op("dve", "tensor_copy", ["mio"], ["miof"], out=miof[:], in_=mio[:])
    for d_ in range(4):
        TSc("dve", maskD[:, d_, :], miof[:], float(128 * d_), ALU.is_ge, ["miof"], ["maskD"])

    kvrr = [0]
    ptrr = [0]
    sbank = [0]
    for j in range(NJ):
        nglob = 32 * j + 32
        for h in range(4):
            chunks = [("g", k0) for k0 in range(0, nglob, 8)] + [("d", 0)]
            ntile_total = nglob + 4
            tcount = 0
            for (kind, k0) in chunks:
                sl = kvrr[0] % NKB
                kvrr[0] += 1
                ntc = 8 if kind == "g" else 4
                if kind == "g":
                    blks = sorted({(k0 + x) // 4 for x in range(8)})
                    P.dma("sp", KTc[sl][:], KT_all[:, h, k0 * 128:(k0 + 8) * 128],
                          [("KTa", b_) for b_ in blks], [("KTc", sl)], "ktc%d" % sl)
                    P.dma("pool", Vc[sl][:], V_all[h, :, k0:k0 + 8, :],
                          [("Va", b_) for b_ in blks], [("Vc", sl)], "vc%d" % sl)
                else:
                    P.dma("sp", KTc[sl][:, 0:512], KT_ownd[:, h, j * 512:(j + 1) * 512],
                          [("KTo", j)], [("KTc", sl)], "ktc%d" % sl)
                    P.dma("pool", Vc[sl][:, 0:4, :], V_ownd[h, :, 4 * j:4 * j + 4, :],
                          [("Vo", j)], [("Vc", sl)], "vc%d" % sl)
                for k in range(ntc):
                    sb_ = (0, 2)[sbank[0] % 2]
                    sbank[0] += 1
                    pt = ptrr[0] % 3
                    ptrr[0] += 1
                    for m in range(2):
                        MM(ps(sb_ + m), KTc[sl][64 * m:64 * m + 64, k * 128:(k + 1) * 128],
                           QT_own[64 * m:64 * m + 64, h, j * 512:(j + 1) * 512], [("KTc", sl), "QT_own"],
                           [("ps", sb_ + m)], tile_position=(64 * m, 0))
                    bias = 0.0
                    if kind == "g" and k0 + k >= 32 * j:
                        r_ = k0 + k - 32 * j
                        bias = cmeta[:, r_:r_ + 1]
                    ACT(PT[pt][:], psum[:, sb_:sb_ + 2, :], AF.Exp, [("ps", sb_), ("ps", sb_ + 1), "cmeta"],
                        [("PT", pt)], scale=0.125, bias=bias)
                    if kind == "d":
                        TT("dve", PT[pt][:], PT[pt][:], maskD[:, k, :].unsqueeze(1).broadcast_to([128, 2, 512]),
                           ALU.mult, [("PT", pt), "maskD"], [("PT", pt)])
                    first = tcount == 0
                    last = tcount == ntile_total - 1
                    for m in range(2):
                        MM(ps(4 + m), Vc[sl][:, k, :], PT[pt][:, m, :], [("Vc", sl), ("PT", pt)], [("ps", 4 + m)],
                           start=first, stop=last)
                    if first:
                        op("dve", "tensor_copy", [("PT", pt)], ["lacc"], out=lacc[:], in_=PT[pt][:])
                    else:
                        TT("dve", lacc[:], lacc[:], PT[pt][:], ALU.add, [("PT", pt), "lacc"], ["lacc"])
                    tcount += 1
            for m in range(2):
                bq_ = 6 + m
                MM(ps(bq_), ones_f[:], lacc[:, m, :], ["lacc", "ones_f"], [("ps", bq_)])
                op("dve", "reciprocal", [("ps", bq_)], [("Bsb", m)], out=Bsb[m][:], in_=ps(bq_))
                TT("dve", tm[m][:], ps(4 + m), Bsb[m][:], ALU.mult, [("ps", 4 + m), ("Bsb", m)], [("tm", m)])
            STT(Dt[:], tm[1][:], neglam, tm[0][:], ALU.mult, ALU.add, [("tm", 0), ("tm", 1), "small"], ["Dt"])
            TT("pool", sq[:], Dt[:], Dt[:], ALU.mult, ["Dt"], ["sq"])
            MM(ps(7), ones_f[:], sq[:], ["sq", "ones_f"], [("ps", 7)])
            TSc("dve", msr[:], ps(7), 1.0 / 128, ALU.mult, [("ps", 7)], ["msr"], s2=EPS, o1=ALU.add)
            TT("pool", msr[:], msr[:], neghalf[:], ALU.pow, ["msr", "neghalf"], ["msr"])
            TT("dve", Dt[:], Dt[:], msr[:], ALU.mult, ["Dt", "msr"], ["Dt"])
            TSc("dve", attnT[:, h, j * 512:(j + 1) * 512], Dt[:], gcol[:, 0:1], ALU.mult, ["Dt", "gcol"], ["attnT"])

    P.barrier()

    A = Alloc(T0, 207 * KB)
    X = A("X", [128, 8, 2, 512], F32)
    Xb = A("Xb", [128, 8, 2, 512], BF16)
    yv = A("yv", [128, 4, 512], F32)
    y2 = A("y2", [128, 512], F32)
    gT = A("gT", [128, 4, 512], BF16)
    sg = A("sg", [128, 512], F32)
    aS = A("aS", [128, 2, 16], F32)
    tmpc1 = A("tmpc1b", [128, 64], F32)
    tmpc2 = A("tmpc2b", [128, 64], F32)
    for j in range(NJ):
        c0 = j * 512
        cmul(aS[:, 0, :], aS[:, 1, :], PW[:, 0, 0, :], PW[:, 0, 1, :], Sown[:, 0, j, :], Sown[:, 1, j, :], [128, 16],
             ["PW", "Sown"], ["aS"])
        for half in range(2):
            for qq in range(8):
                q = half * 8 + qq
                p0 = 32 * (q % 4)
                ct = q // 4
                for ri in range(2):
                    b = nbank((0, 1, 2, 3))
                    MM(ps(b), TB[p0:p0 + 32, ct, 0, ri, :], uT_own[p0:p0 + 32, ct, c0:c0 + 512], ["TB", "uT_own"],
                       [("ps", b)], tile_position=(p0, 0))
                    if (qq + ri) % 2 == 0:
                        op("act", "copy", [("ps", b)], [("X", qq)], out=X[:, qq, ri, :], in_=ps(b))
                    else:
                        op("dve", "tensor_copy", [("ps", b)], [("X", qq)], out=X[:, qq, ri, :], in_=ps(b))
            for ri in range(2):
                TT("dve", X[:, :, ri, 0:1], X[:, :, ri, 0:1], aS[:, ri, half * 8:half * 8 + 8].unsqueeze(2), ALU.add,
                   [("X", qq) for qq in range(8)] + ["aS"], [("X", qq) for qq in range(8)])
            for qq in range(8):
                q = half * 8 + qq
                xk = [("X", qq)]

                def lvl(k, lo_sl, hi_sl):
                    ar = PW[:, k, 0, q:q + 1]
                    ai = PW[:, k, 1, q:q + 1]
                    nai = PW[:, k, 2, q:q + 1]
                    xr, xi = X[:, qq, 0, :], X[:, qq, 1, :]
                    STT(xr[:, hi_sl], xr[:, lo_sl], ar, xr[:, hi_sl], ALU.mult, ALU.add, xk + ["PW"], xk)
                    STT(xr[:, hi_sl], xi[:, lo_sl], nai, xr[:, hi_sl], ALU.mult, ALU.add, xk + ["PW"], xk)
                    STT(xi[:, hi_sl], xi[:, lo_sl], ar, xi[:, hi_sl], ALU.mult, ALU.add, xk + ["PW"], xk)
                    STT(xi[:, hi_sl], xr[:, lo_sl], ai, xi[:, hi_sl], ALU.mult, ALU.add, xk + ["PW"], xk)

                for k in range(9):
                    st2 = 2 << k
                    lvl(k, slice((1 << k) - 1, 512, st2), slice(st2 - 1, 512, st2))
                for k in range(7, -1, -1):
                    st2 = 2 << k
                    lvl(k, slice(st2 - 1, 512 - (1 << k), st2), slice(st2 + (1 << k) - 1, 512, st2))
                op("pool", "tensor_copy", xk, [("Xb", qq)], out=Xb[:, qq, :, :], in_=X[:, qq, :, :])
            for cl in range(2):
                ct = half * 2 + cl
                b = nbank((4, 5))
                for ql in range(4):
                    qq = cl * 4 + ql
                    for ri in range(2):
                        MM(ps(b)[32 * ql:32 * ql + 32, :], CT[:, ct, ri, 32 * ql:32 * ql + 32], Xb[:, qq, ri, :],
                           ["CT", ("Xb", qq)], [("ps", b)], start=ri == 0, stop=ri == 1, tile_position=(0, 32 * ql))
                STT(yv[:, ct, :], uT_own[:, ct, c0:c0 + 512], dcol[:, ct:ct + 1], ps(b), ALU.mult, ALU.add,
                    ["uT_own", "dcol", ("ps", b)], [("yv", ct)])
        for ct in range(4):
            TT("pool", y2[:], yv[:, ct, :], yv[:, ct, :], ALU.mult, [("yv", ct)], ["y2"])
            TSc("dve", y2[:], y2[:], GELU_K2, ALU.mult, ["y2"], ["y2"], s2=GELU_K1, o1=ALU.add)
            TT("dve", y2[:], y2[:], yv[:, ct, :], ALU.mult, ["y2", ("yv", ct)], ["y2"])
            ACT(sg[:], y2[:], AF.Sigmoid, ["y2"], ["sg"])
            TT("dve", gT[:, ct, :], sg[:], yv[:, ct, :], ALU.mult, ["sg", ("yv", ct)], [("gT", ct)])
        for ot in range(4):
            b = nbank((6, 7))
            for ct in range(4):
                MM(ps(b), Wglu_bf[:, ct, ot * 128:(ot + 1) * 128], gT[:, ct, :], ["Wglu", ("gT", ct)], [("ps", b)],
                   start=ct == 0, stop=ct == 3)
            ACT(sg[:], ps(b), AF.Sigmoid, [("ps", b), "bglu"], ["sg"], bias=bglu[:, ot:ot + 1])
            TT("dve", ssmT[:, ot, c0:c0 + 512], sg[:], gT[:, ot, :], ALU.mult, ["sg", ("gT", ot)], ["ssmT"])

    P.barrier()
    dump("attnT", attnT, []); dump("ssmT", ssmT, [])
    P.barrier()

    A = Alloc(T0, 207 * KB)
    h_sb = A("h_sb", [128, 16, D], F32)
    H_END = A.cur
    xs2 = [A("xs2_%d" % i, [128, D], F32) for i in range(2)]
    g2row = A("g2row", [128, D], F32)
    hb16 = A("hb16", [128, D], BF16)
    ss2 = A("ss2", [128, 4], F32)
    Ab = Alloc(8 * KB, OFF["TB"][0])
    h2nT = Ab("h2nT", [128, 8, 2048], BF16)
    Aw = Alloc(OFF["TB"][0], AS_LO)
    Wout_bf = Aw("Wout_bf", [128, 8, D], BF16)
    P.dma("sp", g2row[:], g2row_d.partition_broadcast(128).rearrange("p o f -> p (o f)"), [], ["g2row"], "c1")
    for kt in range(8):
        P.dma("pool", Wout_bf[:, kt, :], w_out[kt * 128:(kt + 1) * 128, :], [], ["Wout"], "wout")
    for tt in range(NTO):
        xb_ = xs2[tt % 2]
        P.dma("sp", xb_[:], xo[tt * 128:(tt + 1) * 128, :], [], [("xs2", tt % 2)], "xs2_%d" % (tt % 2))
        for nh in range(2):
            b = nbank((0, 1, 2, 3))
            for ft in range(8):
                src = attnT if ft < 4 else ssmT
                MM(ps(b), src[:, ft % 4, tt * 128:(tt + 1) * 128], Wout_bf[:, ft, nh * 512:(nh + 1) * 512],
                   ["attnT", "ssmT", "Wout"], [("ps", b)], start=ft == 0, stop=ft == 7)
            TT("dve", h_sb[:, tt, nh * 512:(nh + 1) * 512], ps(b), xb_[:, nh * 512:(nh + 1) * 512], ALU.add,
               [("ps", b), ("xs2", tt % 2)], [("h", tt)])
        ACT(hb16[:], h_sb[:, tt, :], AF.Square, [("h", tt)], ["hb16", "ss2"], accum_out=ss2[:, 0:1])
        TSc("dve", ss2[:, 1:2], ss2[:, 0:1], 1.0 / D, ALU.mult, ["ss2"], ["ss2b"], s2=EPS, o1=ALU.add)
        TT("pool", ss2[:, 1:2], ss2[:, 1:2], neghalf[:, 0:1], ALU.pow, ["ss2b", "neghalf"], ["ss2b"])
        STT(hb16[:], h_sb[:, tt, :], ss2[:, 1:2], g2row[:], ALU.mult, ALU.mult, [("h", tt), "ss2b", "g2row", "hb16"],
            ["hb16"])
        for kt in range(8):
            if kt % 4 == 0:
                b = nbank((4, 5, 6, 7))
            op("pe", "transpose", ["hb16", "ident_bf"], [("ps", b)], out=psbf(b)[:, (kt % 4) * 128:(kt % 4 + 1) * 128],
               in_=hb16[:, kt * 128:(kt + 1) * 128], identity=ident_bf[:])
            if kt % 4 == 3:
                kt0 = kt - 3
                op("act", "copy", [("ps", b)], [("h2nT", tt)],
                   out=h2nT[:, kt0:kt0 + 4, tt * 128:(tt + 1) * 128],
                   in_=psbf(b)[:, 0:512].rearrange("p (k t) -> p k t", k=4))

    P.barrier()
    dump("h_sb", h_sb, []); dump("h2nT", h2nT, [])
    P.barrier()

    Ac = Alloc(OFF["TB"][0], AS_HI)
    NFH = NFF // 2
    actT = Ac("actT", [128, NFH, 2048], BF16)
    Wd_bf = Ac("Wd_bf", [128, NFH, D], BF16)
    Ad = Alloc(H_END, 207 * KB)
    Wg = [Ad("Wg%d" % i, [128, 8, 128], BF16) for i in range(2)]
    Wu = [Ad("Wu%d" % i, [128, 8, 128], BF16) for i in range(2)]
    sgl = [Ad("sgl%d" % i, [128, 512], F32) for i in range(2)]
    fin = [Ad("fin%d" % i, [128, D], F32) for i in range(2)]
    g2r2 = Ad("g2r2", [128, D], F32)
    ss3 = Ad("ss3", [128, 4], F32)
    jk3 = Ad("jk3", [128, D], BF16)
    P.dma("sp", g2r2[:], fgrow_d.partition_broadcast(128).rearrange("p o f -> p (o f)"), [], ["fgrow2"], "c1")
    for fh in range(2):
        for fl in range(NFH):
            P.dma("pool", Wd_bf[:, fl, :], w_down[(fh * NFH + fl) * 128:(fh * NFH + fl + 1) * 128, :],
                  [], [("Wd", fl)], "wd")
        for fl in range(NFH):
            f0 = (fh * NFH + fl) * 128
            wb = fl % 2
            P.dma("pool", Wg[wb][:], w_gate[:, f0:f0 + 128].rearrange("(k p) f -> p k f", p=128), [], [("Wg", wb)],
                  "wg%d" % wb)
            P.dma("pool", Wu[wb][:], w_up[:, f0:f0 + 128].rearrange("(k p) f -> p k f", p=128), [], [("Wu", wb)],
                  "wu%d" % wb)
            for tb in range(SO // 512):
                bg, bu = nbank((0, 1, 2, 3)), nbank((0, 1, 2, 3))
                hk = [("h2nT", tt) for tt in range(4 * tb, 4 * tb + 4)]
                for kt in range(8):
                    MM(ps(bg), Wg[wb][:, kt, :], h2nT[:, kt, tb * 512:(tb + 1) * 512], hk + [("Wg", wb)], [("ps", bg)],
                       start=kt == 0, stop=kt == 7)
                for kt in range(8):
                    MM(ps(bu), Wu[wb][:, kt, :], h2nT[:, kt, tb * 512:(tb + 1) * 512], hk + [("Wu", wb)], [("ps", bu)],
                       start=kt == 0, stop=kt == 7)
                s_ = sgl[tb % 2]
                ACT(s_[:], ps(bg), AF.Silu, [("ps", bg)], [("sgl", tb % 2)])
                TT("dve", actT[:, fl, tb * 512:(tb + 1) * 512], s_[:], ps(bu), ALU.mult, [("sgl", tb % 2), ("ps", bu)],
                   [("actT", fl)])
        for tt in range(NTO):
            for nh in range(2):
                b = nbank((4, 5, 6, 7))
                for fl in range(NFH):
                    MM(ps(b), actT[:, fl, tt * 128:(tt + 1) * 128], Wd_bf[:, fl, nh * 512:(nh + 1) * 512],
                       [("actT", fl), ("Wd", fl)], [("ps", b)], start=fl == 0, stop=fl == NFH - 1)
                TT("dve", h_sb[:, tt, nh * 512:(nh + 1) * 512], ps(b), h_sb[:, tt, nh * 512:(nh + 1) * 512], ALU.add,
                   [("ps", b), ("h", tt)], [("h", tt)])
    for tt in range(NTO):
        f_ = fin[tt % 2]
        ACT(jk3[:], h_sb[:, tt, :], AF.Square, [("h", tt)], ["jk3", "ss3"], accum_out=ss3[:, 0:1])
        TSc("dve", ss3[:, 1:2], ss3[:, 0:1], 1.0 / D, ALU.mult, ["ss3"], ["ss3b"], s2=EPS, o1=ALU.add)
        TT("pool", ss3[:, 1:2], ss3[:, 1:2], neghalf[:, 0:1], ALU.pow, ["ss3b", "neghalf"], ["ss3b"])
        STT(f_[:], h_sb[:, tt, :], ss3[:, 1:2], g2r2[:], ALU.mult, ALU.mult, [("h", tt), "ss3b", "fgrow2"],
            [("fin", tt % 2)])
        P.dma("sp", out_d[tt * 128:(tt + 1) * 128, :], f_[:], [("fin", tt % 2)], [("out", tt)], "outw")

    P.finish(final_sems=["outw"] + (["dbgw"] if dbg else []))
    st.close()
    return nc


_CACHE = {}


def _host_inputs(NJ, x, positions, norm1_g, w_in, lambda_q1, lambda_k1, lambda_q2, lambda_k2, subln_g,
                 ssm_lambda_re, ssm_lambda_im, ssm_log_step, ssm_b_re, ssm_b_im, ssm_c_re, ssm_c_im, ssm_d,
                 ssm_w_glu, ssm_b_glu, w_out, norm2_g, w_gate, w_up, w_down, final_g):
    f = lambda a: np.ascontiguousarray(np.asarray(a, dtype=np.float32))
    S = 4096 * NJ
    NT = S // 128
    x2 = f(x).reshape(S, D)
    pos = np.asarray(positions).reshape(S).astype(np.int32)
    invf = (500000.0 ** (-np.arange(0, 16, 2, dtype=np.float32) / 16)).astype(np.float32)
    lre, lim, lst = f(ssm_lambda_re)[0], f(ssm_lambda_im)[0], f(ssm_log_step)[0]

    def pair(a):
        return np.ascontiguousarray(a.reshape(16, 2, 64).transpose(1, 2, 0).reshape(128, 16))

    cmask = np.zeros((128, 130), np.float32)
    pidx = np.arange(128)
    cols = np.arange(128)
    cmask[:, :128] = ((pidx[:, None] // 64) == ((cols[None, :] // 16) % 2)).astype(np.float32)
    cmask[:, 128] = ((pidx // 16) % 2 == 0)
    cmask[:, 129] = ((pidx // 16) % 2 == 1)
    common = {
        "xa": x2,
        "posa": np.ascontiguousarray(pos.reshape(NT, 128).T),
        "invf": np.ascontiguousarray(np.broadcast_to(invf[None, :], (128, 8))),
        "w_in": f(w_in)[0],
        "g1": np.ascontiguousarray(f(norm1_g)[0].reshape(8, 128).T),
        "lamv": np.concatenate([f(lambda_q1)[0], f(lambda_k1)[0], f(lambda_q2)[0], f(lambda_k2)[0]])[None, :],
        "subg": f(subln_g)[0].reshape(128, 1),
        "lr_nat": np.ascontiguousarray(lre.T), "li_nat": np.ascontiguousarray(lim.T),
        "ls_nat": np.ascontiguousarray(np.broadcast_to(lst[None, :], (64, 32))),
        "lr_pair": pair(lre), "li_pair": pair(lim),
        "ls_pair": pair(np.ascontiguousarray(np.broadcast_to(lst[:, None], (32, 64)))),
        "b_re_nat": np.ascontiguousarray(f(ssm_b_re)[0].transpose(1, 0, 2).reshape(64, 512)),
        "b_im_nat": np.ascontiguousarray(f(ssm_b_im)[0].transpose(1, 0, 2).reshape(64, 512)),
        "c_re_rows": np.ascontiguousarray(f(ssm_c_re)[0].reshape(4, 128, 64).transpose(1, 0, 2)),
        "c_im_rows": np.ascontiguousarray(f(ssm_c_im)[0].reshape(4, 128, 64).transpose(1, 0, 2)),
        "dcol": np.ascontiguousarray(f(ssm_d)[0].reshape(4, 128).T),
        "w_glu": f(ssm_w_glu)[0],
        "bglu": np.ascontiguousarray(f(ssm_b_glu)[0].reshape(4, 128).T),
        "w_out": f(w_out)[0],
        "g2row": f(norm2_g)[0][None, :],
        "w_gate": f(w_gate)[0], "w_up": f(w_up)[0], "w_down": f(w_down)[0],
        "fgrow": f(final_g)[None, :],
        "cmask": cmask,
    }
    maps = []
    for c in range(NCORES):
        rows = np.concatenate([np.arange((8 * j + c) * 512, (8 * j + c + 1) * 512) for j in range(NJ)])
        cm = np.zeros((128, 64), np.float32)
        for r in range(32):
            cm[:, r] = 0.0 if r < 4 * c else NEG
        cm[:, 32 + c] = 1.0
        m = dict(common)
        m["xo"] = np.ascontiguousarray(x2[rows])
        m["poso"] = np.ascontiguousarray(pos[rows].reshape(4 * NJ, 128).T)
        m["cmeta"] = cm
        maps.append(m)
    return maps


DBG = False
LAST = {}


def kernel(**inputs):
    x = np.asarray(inputs["x"])
    S = x.shape[1]
    NJ = S // 4096
    if NJ not in _CACHE:
        _CACHE[NJ] = build(NJ, dbg=DBG)
    nc = _CACHE[NJ]
    maps = _host_inputs(NJ, **inputs)
    res = run_bass_kernel_spmd(nc, maps, core_ids=list(range(NCORES)))
    if DBG:
        LAST["res"] = res.results
    out = np.zeros((S, D), np.float32)
    for c in range(NCORES):
        o = np.asarray(res.results[c]["out"])
        for j in range(NJ):
            out[(8 * j + c) * 512:(8 * j + c + 1) * 512] = o[j * 512:(j + 1) * 512]
    return out.reshape(1, S, D)
```
